# Optimizing a Trainium2 kernel written in Bass

```python
import math
import jax, jax.numpy as jnp
from jax import lax
import numpy as np

D_MODEL = 1024
BATCH = 2
SEQ = 16384
DEPTH = 1
DEC_BATCH = 4
DEC_SEQ = 8192
PAST_LEN = 128

SSM_WIDTH = D_MODEL // 2
SSM_GROUP_CH = 16
SSM_GROUPS = SSM_WIDTH // SSM_GROUP_CH
SSM_STATE = 64
DT_MIN = 1e-3
DT_MAX = 1e-1
N_HEADS = 8
QK_NOPE = 64
QK_ROPE = 32
V_HEAD = 64
Q_LORA = D_MODEL // 4
KV_LORA = D_MODEL // 8
ATTN_WIDTH = N_HEADS * V_HEAD
ROPE_THETA = 10000.0
Q_BLOCK = 128
LN_EPS = 1e-5
RMS_EPS = 1e-6
ALPHA = (2.0 * DEPTH) ** 0.25
BETA = (8.0 * DEPTH) ** -0.25

IN_SIZES = (SSM_WIDTH, SSM_WIDTH, Q_LORA, KV_LORA, QK_ROPE, ATTN_WIDTH, D_MODEL, D_MODEL)
IN_WIDTH = sum(IN_SIZES)
IN_SPLITS = tuple(int(v) for v in np.cumsum(IN_SIZES)[:-1])

kernel_name = 'hybrid_s5_mla_gated_encoder'

F32 = jnp.float32


def _layer_norm(x, g, b):
    xf = x.astype(F32)
    mu = jnp.mean(xf, -1, keepdims=True)
    var = jnp.mean(jnp.square(xf - mu), -1, keepdims=True)
    y = (xf - mu) * lax.rsqrt(var + LN_EPS) * g.astype(F32) + b.astype(F32)
    return y.astype(x.dtype)


def _rms_norm(x, g):
    xf = x.astype(F32)
    y = xf * lax.rsqrt(jnp.mean(xf * xf, -1, keepdims=True) + RMS_EPS) * g.astype(F32)
    return y.astype(x.dtype)


def _rope(x, pos):
    half = x.shape[-1] // 2
    inv = ROPE_THETA ** (-jnp.arange(half, dtype=F32) * 2.0 / x.shape[-1])
    ang = pos[:, None] * inv[None, :]
    cos = jnp.cos(ang)[:, None, :]
    sin = jnp.sin(ang)[:, None, :]
    xf = x.astype(F32)
    x1, x2 = xf[..., :half], xf[..., half:]
    return jnp.concatenate([x1 * cos - x2 * sin, x1 * sin + x2 * cos], -1).astype(x.dtype)


def _linear_scan_op(left, right):
    a_l, b_l = left
    a_r, b_r = right
    return a_l * a_r, a_r * b_l + b_r


def _s5_direction(bu, a_re, a_im, log_dt, c_re, c_im, reverse):
    lam = lax.complex(a_re.astype(F32), a_im.astype(F32))
    dt = jnp.exp(log_dt.astype(F32))[:, None]
    lam_bar = jnp.exp(lam * dt)
    b_seq = bu * ((lam_bar - 1.0) / lam)
    a_seq = jnp.broadcast_to(lam_bar, b_seq.shape)
    _, h = lax.associative_scan(_linear_scan_op, (a_seq, b_seq), axis=1, reverse=reverse)
    c = lax.complex(c_re.astype(F32), c_im.astype(F32))
    return jnp.real(jnp.einsum('blgn,gpn->blgp', h, c))


def _s5_branch(u, b_re, b_im, a_re_fwd, a_im_fwd, log_dt_fwd, c_re_fwd, c_im_fwd,
               a_re_bwd, a_im_bwd, log_dt_bwd, c_re_bwd, c_im_bwd, d_skip, w_glu, b_glu):
    bsz, L, _ = u.shape
    ug = u.astype(F32).reshape(bsz, L, SSM_GROUPS, SSM_GROUP_CH)
    bu = lax.complex(jnp.einsum('blgp,gnp->blgn', ug, b_re.astype(F32)),
                     jnp.einsum('blgp,gnp->blgn', ug, b_im.astype(F32)))
    y = (_s5_direction(bu, a_re_fwd, a_im_fwd, log_dt_fwd, c_re_fwd, c_im_fwd, False)
         + _s5_direction(bu, a_re_bwd, a_im_bwd, log_dt_bwd, c_re_bwd, c_im_bwd, True)
         + d_skip.astype(F32).reshape(SSM_GROUPS, SSM_GROUP_CH) * ug)
    y = jax.nn.gelu(y.reshape(bsz, L, SSM_WIDTH)).astype(u.dtype)
    return y * jax.nn.sigmoid(y @ w_glu + b_glu)


def _mla_branch(c_q, c_kv, k_rope_in, q_norm_g, w_uq, kv_norm_g, w_ukv):
    bsz, L, _ = c_q.shape
    pos = jnp.arange(L, dtype=F32)
    q = (_rms_norm(c_q, q_norm_g) @ w_uq).reshape(bsz, L, N_HEADS, QK_NOPE + QK_ROPE)
    q_nope = q[..., :QK_NOPE]
    q_rope = _rope(q[..., QK_NOPE:], pos)
    kv = (_rms_norm(c_kv, kv_norm_g) @ w_ukv).reshape(bsz, L, N_HEADS, QK_NOPE + V_HEAD)
    k_nope = kv[..., :QK_NOPE]
    v = kv[..., QK_NOPE:]
    k_rope = _rope(k_rope_in[:, :, None, :], pos)[:, :, 0, :]
    scale = (QK_NOPE + QK_ROPE) ** -0.5
    n_blk = L // Q_BLOCK
    qn_blocks = q_nope.reshape(bsz, n_blk, Q_BLOCK, N_HEADS, QK_NOPE).transpose(1, 0, 2, 3, 4)
    qr_blocks = q_rope.reshape(bsz, n_blk, Q_BLOCK, N_HEADS, QK_ROPE).transpose(1, 0, 2, 3, 4)

    def attend(blk):
        qn_b, qr_b = blk
        s = (jnp.einsum('bqhd,bkhd->bhqk', qn_b, k_nope, preferred_element_type=F32)
             + jnp.einsum('bqhr,bkr->bhqk', qr_b, k_rope, preferred_element_type=F32))
        p = jax.nn.softmax(s * scale, axis=-1)
        return jnp.einsum('bhqk,bkhd->bqhd', p.astype(v.dtype), v)

    out = lax.map(attend, (qn_blocks, qr_blocks))
    return out.transpose(1, 0, 2, 3, 4).reshape(bsz, L, ATTN_WIDTH)


def _layer(x, w_in, ssm_b_re, ssm_b_im, ssm_a_re_fwd, ssm_a_im_fwd, ssm_log_dt_fwd,
           ssm_c_re_fwd, ssm_c_im_fwd, ssm_a_re_bwd, ssm_a_im_bwd, ssm_log_dt_bwd,
           ssm_c_re_bwd, ssm_c_im_bwd, ssm_d, w_glu, b_glu, q_norm_g, w_uq, kv_norm_g,
           w_ukv, w_branch_ssm, w_branch_attn, w_o, ln_g, ln_b):
    h = x @ w_in
    u, z_s, c_q, c_kv, k_r, z_a, g_s, g_a = jnp.split(h, IN_SPLITS, axis=-1)
    y_s = _s5_branch(u, ssm_b_re, ssm_b_im, ssm_a_re_fwd, ssm_a_im_fwd, ssm_log_dt_fwd,
                     ssm_c_re_fwd, ssm_c_im_fwd, ssm_a_re_bwd, ssm_a_im_bwd, ssm_log_dt_bwd,
                     ssm_c_re_bwd, ssm_c_im_bwd, ssm_d, w_glu, b_glu) * jax.nn.silu(z_s)
    y_a = _mla_branch(c_q, c_kv, k_r, q_norm_g, w_uq, kv_norm_g, w_ukv) * jax.nn.silu(z_a)
    merged = jax.nn.sigmoid(g_s) * (y_s @ w_branch_ssm) + jax.nn.sigmoid(g_a) * (y_a @ w_branch_attn)
    return _layer_norm(ALPHA * x + merged @ w_o, ln_g, ln_b)


def _trunk(x, ln_in_g, ln_in_b, weights):
    x = _layer_norm(x, ln_in_g, ln_in_b)
    for l in range(DEPTH):
        x = _layer(x, *[w[l] for w in weights])
    return x


def setup_inputs(seed: int = 0) -> dict:
    key = jax.random.key(seed)
    ks = jax.random.split(key, 40)

    def nrm(k, shape, scale):
        return jax.random.normal(k, shape, F32) * scale

    G, N, P = SSM_GROUPS, SSM_STATE, SSM_GROUP_CH
    n_idx = jnp.arange(N, dtype=F32)

    def a_re(k):
        return -0.5 + nrm(k, (DEPTH, G, N), 0.01)

    def a_im(k):
        return jnp.pi * n_idx + nrm(k, (DEPTH, G, N), 0.01)

    def log_dt(k):
        return jax.random.uniform(k, (DEPTH, G), F32, math.log(DT_MIN), math.log(DT_MAX))

    return {
        'x_prompt': nrm(ks[0], (BATCH, SEQ, D_MODEL), 1.0),
        'x_sample': nrm(ks[1], (DEC_BATCH, DEC_SEQ, D_MODEL), 1.0),
        'ln_in_g': 1.0 + nrm(ks[2], (D_MODEL,), 0.01),
        'ln_in_b': nrm(ks[3], (D_MODEL,), 0.01),
        'w_in': nrm(ks[4], (DEPTH, D_MODEL, IN_WIDTH), D_MODEL ** -0.5),
        'ssm_b_re': nrm(ks[5], (DEPTH, G, N, P), (2.0 * P) ** -0.5),
        'ssm_b_im': nrm(ks[6], (DEPTH, G, N, P), (2.0 * P) ** -0.5),
        'ssm_a_re_fwd': a_re(ks[7]),
        'ssm_a_im_fwd': a_im(ks[8]),
        'ssm_log_dt_fwd': log_dt(ks[9]),
        'ssm_c_re_fwd': nrm(ks[10], (DEPTH, G, P, N), (2.0 * N) ** -0.5),
        'ssm_c_im_fwd': nrm(ks[11], (DEPTH, G, P, N), (2.0 * N) ** -0.5),
        'ssm_a_re_bwd': a_re(ks[12]),
        'ssm_a_im_bwd': a_im(ks[13]),
        'ssm_log_dt_bwd': log_dt(ks[14]),
        'ssm_c_re_bwd': nrm(ks[15], (DEPTH, G, P, N), (2.0 * N) ** -0.5),
        'ssm_c_im_bwd': nrm(ks[16], (DEPTH, G, P, N), (2.0 * N) ** -0.5),
        'ssm_d': nrm(ks[17], (DEPTH, SSM_WIDTH), 1.0),
        'w_glu': nrm(ks[18], (DEPTH, SSM_WIDTH, SSM_WIDTH), SSM_WIDTH ** -0.5),
        'b_glu': nrm(ks[19], (DEPTH, SSM_WIDTH), 0.01),
        'q_norm_g': 1.0 + nrm(ks[20], (DEPTH, Q_LORA), 0.01),
        'w_uq': nrm(ks[21], (DEPTH, Q_LORA, N_HEADS * (QK_NOPE + QK_ROPE)), Q_LORA ** -0.5),
        'kv_norm_g': 1.0 + nrm(ks[22], (DEPTH, KV_LORA), 0.01),
        'w_ukv': nrm(ks[23], (DEPTH, KV_LORA, N_HEADS * (QK_NOPE + V_HEAD)), KV_LORA ** -0.5),
        'w_branch_ssm': nrm(ks[24], (DEPTH, SSM_WIDTH, D_MODEL), BETA * SSM_WIDTH ** -0.5),
        'w_branch_attn': nrm(ks[25], (DEPTH, ATTN_WIDTH, D_MODEL), BETA * ATTN_WIDTH ** -0.5),
        'w_o': nrm(ks[26], (DEPTH, D_MODEL, D_MODEL), BETA * D_MODEL ** -0.5),
        'ln_g': 1.0 + nrm(ks[27], (DEPTH, D_MODEL), 0.01),
        'ln_b': nrm(ks[28], (DEPTH, D_MODEL), 0.01),
    }


def reference(x_prompt, x_sample, ln_in_g, ln_in_b, w_in, ssm_b_re, ssm_b_im,
              ssm_a_re_fwd, ssm_a_im_fwd, ssm_log_dt_fwd, ssm_c_re_fwd, ssm_c_im_fwd,
              ssm_a_re_bwd, ssm_a_im_bwd, ssm_log_dt_bwd, ssm_c_re_bwd, ssm_c_im_bwd,
              ssm_d, w_glu, b_glu, q_norm_g, w_uq, kv_norm_g, w_ukv,
              w_branch_ssm, w_branch_attn, w_o, ln_g, ln_b):
    weights = (w_in, ssm_b_re, ssm_b_im, ssm_a_re_fwd, ssm_a_im_fwd, ssm_log_dt_fwd,
               ssm_c_re_fwd, ssm_c_im_fwd, ssm_a_re_bwd, ssm_a_im_bwd, ssm_log_dt_bwd,
               ssm_c_re_bwd, ssm_c_im_bwd, ssm_d, w_glu, b_glu, q_norm_g, w_uq, kv_norm_g,
               w_ukv, w_branch_ssm, w_branch_attn, w_o, ln_g, ln_b)
    y_prompt = _trunk(x_prompt, ln_in_g, ln_in_b, weights)
    y_sample = _trunk(x_sample, ln_in_g, ln_in_b, weights)
    return (y_prompt, y_sample)
```

```python
import contextlib
import numpy as np
import concourse.bass as bass
import concourse.mybir as mybir
from concourse.bass_utils import run_bass_kernel_spmd

F32 = mybir.dt.float32
BF16 = mybir.dt.bfloat16
I32 = mybir.dt.int32
AF = mybir.ActivationFunctionType
ALU = mybir.AluOpType

D_MODEL = 1024
IN_W = 4000
LN_EPS = 1e-5
RMS_EPS = 1e-6
ALPHA = 2.0 ** 0.25
SCALE = 96.0 ** -0.5
TWO_PI_HI = 6.28125
TWO_PI_LO = 0.0019353071795864769
GELU_K = 1.5957691216057308
PIECES = [("u", 0, 512), ("zs", 512, 512), ("cq", 1024, 256), ("ckv", 1280, 128), ("kr", 1408, 32),
          ("za", 1440, 512), ("gs", 1952, 1024), ("ga", 2976, 1024), ("krr", 4000, 32)]
PCOL = {n: c for n, c, w in PIECES}
KV_SETS = np.concatenate([np.arange(7, -1, -1), np.arange(0, 8), np.arange(1, 9), np.arange(8, 0, -1),
                          -np.arange(0, 8)]).astype(np.float32)

ENGS = ("sp", "act", "pool", "pe", "dve")


class Buf:
    __slots__ = ("name", "lw", "rd", "war")

    def __init__(self, name):
        self.name = name
        self.lw = {}
        self.rd = {}
        self.war = {}


class X:
    __slots__ = ("ap", "b", "sh")

    def __init__(self, ap, b, sh=False):
        self.ap = ap
        self.b = b
        self.sh = sh

    def __getitem__(self, idx):
        return X(self.ap[idx], self.b, self.sh)

    def bc(self, axis, shape):
        return X(self.ap.unsqueeze(axis).broadcast_to(list(shape)), self.b, self.sh)

    def re(self, s, **kw):
        return X(self.ap.rearrange(s, **kw), self.b, self.sh)

    def s(self):
        return X(self.ap, self.b, True)


class T:
    def __init__(self, kb, name, shape, dt, space="sbuf"):
        self.t = kb.tile(name, shape, dt, space)
        self.b = Buf(name)

    def __getitem__(self, idx):
        return X(self.t[idx], self.b)


class D:
    def __init__(self, nc, name, shape, dt, kind="Internal"):
        self.t = nc.dram_tensor(name, list(shape), dt, kind=kind)
        self.a = self.t.ap()
        self.b = Buf(name)

    def __getitem__(self, idx):
        return X(self.a[idx], self.b)


class KB:
    NDMA = {"sp": 22, "pool": 4, "act": 2}

    def __init__(self, nc):
        self.nc = nc
        self.stack = contextlib.ExitStack()
        self.stages = []
        self.ops = {e: [] for e in ENGS}
        self.cnt = {e: 0 for e in ENGS}
        self.known = {e: {} for e in ENGS}
        self.sems = {}
        self.dcnt = {}
        self.rr = {q: 0 for q in self.NDMA}
        for e in ("act", "pool", "pe", "dve"):
            self.sems["c_" + e] = self.stack.enter_context(nc.semaphore("c_" + e))
        for q, n in self.NDMA.items():
            for i in range(n):
                k = f"d_{q}{i}"
                self.sems[k] = self.stack.enter_context(nc.semaphore(k))
                self.dcnt[k] = 0
        self.nops = 0
        self.uid = 0

    def push(self):
        self.stages.append(contextlib.ExitStack())

    def pop(self):
        self.flush()
        self.stages.pop().close()

    def tile(self, name, shape, dt, space="sbuf"):
        f = self.nc.sbuf_tensor if space == "sbuf" else self.nc.psum_tensor
        self.uid += 1
        st = self.stages[-1] if self.stages else self.stack
        return st.enter_context(f(f"{name}_{self.uid}", list(shape), dt))

    def _deps(self, eng, reads, writes):
        deps = {}

        def add(k, v):
            if deps.get(k, 0) < v:
                deps[k] = v
        for b in reads:
            for k, v in b.lw.items():
                add(k, v)
        for w in writes:
            b, sh = (w.b, w.sh) if isinstance(w, X) else (w, False)
            if sh:
                if b.rd:
                    b.war = dict(b.lw)
                    for k, v in b.rd.items():
                        if b.war.get(k, 0) < v:
                            b.war[k] = v
                    b.lw = {}
                    b.rd = {}
                for k, v in b.war.items():
                    add(k, v)
            else:
                for dct in (b.lw, b.rd, b.war):
                    for k, v in dct.items():
                        add(k, v)
        waits = []
        for k, v in deps.items():
            if k == "c_" + eng and eng == "pe":
                continue
            if self.known[eng].get(k, 0) >= v:
                continue
            self.known[eng][k] = v
            waits.append((k, v))
        return waits

    def _post(self, ev, reads, writes):
        k, v = ev
        for w in writes:
            b, sh = (w.b, w.sh) if isinstance(w, X) else (w, False)
            if sh:
                if b.lw.get(k, 0) < v:
                    b.lw[k] = v
            else:
                b.lw = {k: v}
                b.rd = {}
                b.war = {k: v}
        for b in reads:
            if b.rd.get(k, 0) < v:
                b.rd[k] = v

    def op(self, eng, fn, reads=(), writes=()):
        waits = self._deps(eng, reads, writes)
        self.cnt[eng] += 1
        ev = ("c_" + eng, self.cnt[eng])
        self.ops[eng].append((fn, waits, ev[0], 1))
        self._post(ev, reads, writes)
        self.nops += 1

    def dma(self, out, in_, q="sp", **kw):
        i = self.rr[q]
        self.rr[q] = (i + 1) % self.NDMA[q]
        k = f"d_{q}{i}"
        reads, writes = [in_.b], [out]
        waits = self._deps(q, reads, writes)
        prev = self.dcnt[k]
        if prev > 0 and self.known[q].get(k, 0) < prev:
            self.known[q][k] = prev
            waits.append((k, prev))
        self.dcnt[k] = prev + 16
        ev = (k, prev + 16)
        oa, ia = out.ap, in_.ap
        self.ops[q].append((lambda e: e.dma_start(out=oa, in_=ia, **kw), waits, k, 16))
        self._post(ev, reads, writes)
        self.nops += 1

    def barrier(self):
        tgt = {}
        for e in ("act", "pool", "pe", "dve"):
            if self.cnt[e] > 0:
                tgt["c_" + e] = self.cnt[e]
        for k, v in self.dcnt.items():
            if v > 0:
                tgt[k] = v
        for e in ENGS:
            waits = []
            for k, v in tgt.items():
                if k == "c_" + e:
                    continue
                if self.known[e].get(k, 0) >= v:
                    continue
                self.known[e][k] = v
                waits.append((k, v))
            if waits:
                self.ops[e].append((None, waits, None, 0))

    def flush(self):
        self.barrier()
        nc, ops, sems = self.nc, self.ops, self.sems

        def run(e, lst):
            for fn, waits, sk, inc in lst:
                for k, v in waits:
                    e.wait_ge(sems[k], v)
                if fn is not None:
                    fn(e).then_inc(sems[sk], inc)
        with nc.Block() as block:
            @block.sync
            def _(e):
                run(e, ops["sp"])

            @block.scalar
            def _(e):
                run(e, ops["act"])

            @block.gpsimd
            def _(e):
                run(e, ops["pool"])

            @block.tensor
            def _(e):
                run(e, ops["pe"])

            @block.vector
            def _(e):
                run(e, ops["dve"])
        self.ops = {e: [] for e in ENGS}

    def close(self):
        self.stack.close()

    def mm(self, out, lhsT, rhs, start=True, stop=True):
        o, l, r = out.ap, lhsT.ap, rhs.ap
        self.op("pe", lambda e: e.matmul(o, lhsT=l, rhs=r, start=start, stop=stop),
                reads=[lhsT.b, rhs.b], writes=[out])

    def tr(self, out, in_, ident):
        o, i, d = out.ap, in_.ap, ident.ap
        self.op("pe", lambda e: e.transpose(out=o, in_=i, identity=d), reads=[in_.b, ident.b], writes=[out])

    def act(self, out, in_, func, bias=None, scale=1.0):
        o, i = out.ap, in_.ap
        rd = [in_.b]
        kw = {}
        if bias is not None:
            if isinstance(bias, X):
                kw["bias"] = bias.ap
                rd.append(bias.b)
            else:
                kw["bias"] = bias
        if isinstance(scale, X):
            rd.append(scale.b)
            scale = scale.ap
        self.op("act", lambda e: e.activation(out=o, in_=i, func=func, scale=scale, **kw), reads=rd, writes=[out])

    def tt(self, eng, out, in0, in1, op):
        o, a, b = out.ap, in0.ap, in1.ap
        self.op(eng, lambda e: e.tensor_tensor(out=o, in0=a, in1=b, op=op), reads=[in0.b, in1.b], writes=[out])

    def ts(self, eng, out, in0, s1, s2, op0, op1=None):
        o, a = out.ap, in0.ap
        rd = [in0.b]
        if isinstance(s1, X):
            rd.append(s1.b)
            s1 = s1.ap
        if isinstance(s2, X):
            rd.append(s2.b)
            s2 = s2.ap
        if op1 is None:
            self.op(eng, lambda e: e.tensor_scalar(out=o, in0=a, scalar1=s1, scalar2=None, op0=op0), reads=rd,
                    writes=[out])
        else:
            self.op(eng, lambda e: e.tensor_scalar(out=o, in0=a, scalar1=s1, scalar2=s2, op0=op0, op1=op1),
                    reads=rd, writes=[out])

    def stt(self, eng, out, in0, scalar, in1, op0, op1):
        o, a, b = out.ap, in0.ap, in1.ap
        rd = [in0.b, in1.b]
        if isinstance(scalar, X):
            rd.append(scalar.b)
            scalar = scalar.ap
        self.op(eng, lambda e: e.scalar_tensor_tensor(out=o, in0=a, scalar=scalar, in1=b, op0=op0, op1=op1),
                reads=rd, writes=[out])

    def cp(self, eng, out, in_):
        o, i = out.ap, in_.ap
        if eng == "act":
            self.op("act", lambda e: e.activation(out=o, in_=i, func=AF.Copy), reads=[in_.b], writes=[out])
        else:
            self.op(eng, lambda e: e.tensor_copy(out=o, in_=i), reads=[in_.b], writes=[out])

    def memset(self, eng, out, val):
        o = out.ap
        self.op(eng, lambda e: e.memset(o, val), writes=[out])

    def recip(self, out, in_):
        o, i = out.ap, in_.ap
        self.op("dve", lambda e: e.reciprocal(out=o, in_=i), reads=[in_.b], writes=[out])

    def scan(self, out, d0, d1, initial=0.0):
        o, a, b = out.ap, d0.ap, d1.ap
        self.op("dve", lambda e: e.tensor_tensor_scan(out=o, data0=a, data1=b, initial=initial, op0=ALU.mult,
                                                      op1=ALU.add), reads=[d0.b, d1.b], writes=[out])

    def bn_stats(self, out, in_):
        o, i = out.ap, in_.ap
        self.op("dve", lambda e: e.bn_stats(out=o, in_=i), reads=[in_.b], writes=[out])

    def bn_aggr(self, out, in_):
        o, i = out.ap, in_.ap
        self.op("dve", lambda e: e.bn_aggr(out=o, in_=i), reads=[in_.b], writes=[out])


class Ring:
    def __init__(self, tiles):
        self.tiles = tiles
        self.i = 0

    def next(self):
        t = self.tiles[self.i % len(self.tiles)]
        self.i += 1
        return t


class Cfg:
    def __init__(self, LA=16384, LB=8192, debug=False):
        self.L = [LA, LB, LB]
        self.nC = [l // 8 for l in self.L]
        self.nCo = [c // 4 for c in self.nC]
        self.Lo = [l // 4 for l in self.L]
        self.NB = list(self.nCo)
        for nb in self.NB:
            assert nb % 128 == 0 and nb <= 512
        self.debug = debug


def sincos(kb, eng2, ang, s_out, c_out, tf, ti, tr, th, r_out=None):
    kb.ts("dve", tf, ang, 1.0 / (2 * np.pi), None, ALU.mult)
    kb.cp("dve", ti, tf)
    kb.cp("dve", tf, ti)
    kb.stt("dve", tr, tf, -TWO_PI_HI, ang, ALU.mult, ALU.add)
    kb.stt("dve", tr, tf, -TWO_PI_LO, tr, ALU.mult, ALU.add)
    kb.ts("dve", tr, tr, 3.141592, -3.141592, ALU.min, ALU.max)
    if r_out is not None:
        kb.cp(eng2, r_out, tr)
    if s_out is not None:
        kb.act(s_out, tr, AF.Sin)
        kb.act(th, tr, AF.Sin, scale=0.5)
        if eng2 == "act":
            kb.act(th, th, AF.Square)
            kb.act(c_out, th, AF.Identity, scale=-2.0, bias=1.0)
        else:
            kb.tt(eng2, th, th, th, ALU.mult)
            kb.ts(eng2, c_out, th, -2.0, 1.0, ALU.mult, ALU.add)


def build(cfg):
    nc = bass.Bass("TRN2", target_bir_lowering=False)
    dbg = cfg.debug
    skind = "ExternalOutput" if dbg else "Internal"
    I = {}

    def inp(name, shape):
        I[name] = D(nc, name, shape, F32, kind="ExternalInput")
        return I[name]
    xr = [inp(f"xr{s}", [cfg.L[s], D_MODEL]) for s in range(3)]
    pos = [inp(f"pos{s}", [1, cfg.L[s]]) for s in range(3)]
    mfm = [inp(f"mf{s}", [1, cfg.nC[s]]) for s in range(3)]
    mbm = [inp(f"mb{s}", [1, cfg.nC[s]]) for s in range(3)]
    w_in = inp("w_in", [D_MODEL, IN_W])
    lng_col = inp("lng_col", [128, 8])
    lnb_col2 = inp("lnb_col2", [128, 8, 2])
    lng_row = inp("lng_row", [1, D_MODEL])
    lnb_row = inp("lnb_row", [1, D_MODEL])
    lam = inp("lam", [64, 4, 32])
    ldt = inp("ldt", [64, 2, 32])
    lamp = inp("lamp", [128, 4, 16])
    ldtp = inp("ldtp", [128, 2, 16])
    bT = inp("bT", [64, 2, 32, 16])
    cT = inp("cT", [64, 4, 32, 16])
    dtile = inp("dtile", [128, 32])
    kvc = inp("kvc", [64, 32, 40])
    maskF = inp("maskF", [128, 128])
    maskB = inp("maskB", [128, 128])
    iden = inp("iden", [128, 128])
    iotaf = inp("iotaf", [64, 2048])
    invf = inp("invf", [128, 1])
    sel = inp("sel", [128, 64])
    w_glu = inp("w_glu", [512, 512])
    bglu_col = inp("bglu_col", [128, 4])
    qg_col = inp("qg_col", [128, 2])
    w_uq = inp("w_uq", [256, 768])
    kvg_col = inp("kvg_col", [128, 1])
    w_ukv = inp("w_ukv", [128, 1024])
    w_bs = inp("w_bs", [512, 1024])
    w_ba = inp("w_ba", [512, 1024])
    w_o = inp("w_o", [1024, 1024])
    lno_g = inp("lno_g", [1, D_MODEL])
    lno_b = inp("lno_b", [1, D_MODEL])
    outs = [D(nc, f"out{s}", [cfg.Lo[s], D_MODEL], F32, kind="ExternalOutput") for s in range(3)]

    WD = D(nc, "WD", [128, 8, IN_W + 32], BF16, skind)
    WSD = D(nc, "WSD", [128, 32, 2, 2, 64], BF16, skind)
    TOED = D(nc, "TOED", [128, 32, 128], BF16, skind)
    CLD = D(nc, "CLD", [64, 32, 2, 2, 128], BF16, skind)
    uD = [D(nc, f"uD{s}", [4, 8, 128, cfg.nC[s]], BF16, skind) for s in range(3)]
    kvD = [D(nc, f"kvD{s}", [128, cfg.L[s]], BF16, skind) for s in range(3)]
    krD = [D(nc, f"krD{s}", [32, cfg.L[s]], BF16, skind) for s in range(3)]
    qD = [D(nc, f"qD{s}", [8, 96, cfg.Lo[s]], BF16, skind) for s in range(3)]
    yD = [D(nc, f"yD{s}", [4, 8, 128, cfg.nCo[s]], BF16, skind) for s in range(3)]
    yaD = [D(nc, f"yaD{s}", [8, 64, cfg.Lo[s]], BF16, skind) for s in range(3)]

    kb = KB(nc)
    ident = T(kb, "ident", [128, 128], F32)
    identb = T(kb, "identb", [128, 128], BF16)
    hbT = T(kb, "hbT", [128, 40], F32)
    epsl = T(kb, "epsl", [128, 1], F32)
    epsr = T(kb, "epsr", [128, 1], F32)
    invc = T(kb, "invc", [128, 1], F32)
    rho8 = T(kb, "rho8", [64, 2, 32], F32)
    phi8 = T(kb, "phi8", [64, 2, 32], F32)
    rho8p = T(kb, "rho8p", [128, 2, 16], F32)
    phi8p = T(kb, "phi8p", [128, 2, 16], F32)
    wuq = T(kb, "wuq", [128, 2, 768], BF16)
    wuqr = T(kb, "wuqr", [128, 2, 768], BF16)
    wukv = T(kb, "wukv", [128, 1024], BF16)
    bglu = T(kb, "bglu", [128, 4], F32)
    ones_f = T(kb, "ones_f", [128, 128], F32)

    kb.dma(ident[:, :], iden[:, :])
    kb.dma(identb[:, :], iden[:, :], q="pool")
    kb.dma(invc[:, :], invf[:, :])
    kb.dma(bglu[:, :], bglu_col[:, :])
    kb.memset("dve", epsl[:, :], LN_EPS)
    kb.memset("dve", epsr[:, :], RMS_EPS)
    kb.memset("pool", ones_f[:, :], 1.0)
    kb.memset("pool", hbT[:, :], 0.0)

    SLOTS = []
    for n, c0, w in PIECES:
        for sub in range((w + 127) // 128):
            SLOTS.append((n, sub, c0 + sub * 128, min(128, w - sub * 128)))
    SLOT = {(n, sub): i for i, (n, sub, c, w) in enumerate(SLOTS)}

    kb.push()
    psP = Ring([T(kb, f"psP{i}", [128, 512], F32, "psum") for i in range(4)])
    gcol = T(kb, "gcol", [128, 8], F32)
    bcol2 = T(kb, "bcol2", [128, 8, 2], F32)
    kb.dma(gcol[:, :], lng_col[:, :])
    kb.dma(bcol2[:, :, :], lnb_col2[:, :, :])
    wfR = Ring([T(kb, f"wf{i}", [128, 8, 128], F32) for i in range(3)])
    wbR = Ring([T(kb, f"wb{i}", [128, 8, 128], BF16) for i in range(3)])
    w_in_v = w_in.a.rearrange("(k p) c -> p k c", p=128)
    wf_kr = T(kb, "wf_kr", [128, 8, 32], F32)
    for si, (n, sub, c0, w) in enumerate(SLOTS):
        wf = wfR.next()
        if n != "krr":
            kb.dma(wf[:, :, 0:w], X(w_in_v[:, :, c0:c0 + w], w_in.b))
            if n == "kr":
                kb.cp("pool", wf_kr[:, :, :], wf[:, :, 0:32])
        else:
            kb.ts("pool", wf[:, :, 0:16], wf_kr[:, :, 16:32], -1.0, None, ALU.mult)
            kb.cp("pool", wf[:, :, 16:32], wf_kr[:, :, 0:16])
        ps = psP.next()
        for k in range(8):
            kb.mm(ps[0:w, 0:2], wf[:, k, 0:w], bcol2[:, k, :], start=(k == 0), stop=(k == 7))
        kb.cp("act", hbT[0:w, si:si + 1].s(), ps[0:w, 0:1])
        wb = wbR.next()
        kb.tt("dve", wb[:, :, 0:w], wf[:, :, 0:w], gcol[:, :].bc(2, [128, 8, w]), ALU.mult)
        kb.dma(WD[:, :, c0:c0 + w].s(), wb[:, :, 0:w])
    qgc = T(kb, "qgc", [128, 2], F32)
    kvgc = T(kb, "kvgc", [128, 1], F32)
    wuqf = T(kb, "wuqf", [128, 2, 768], F32)
    wukvf = T(kb, "wukvf", [128, 1024], F32)
    kb.dma(qgc[:, :], qg_col[:, :])
    kb.dma(kvgc[:, :], kvg_col[:, :])
    kb.dma(wuqf[:, :, :], X(w_uq.a.rearrange("(k p) c -> p k c", p=128), w_uq.b))
    kb.dma(wukvf[:, :], w_ukv[:, :])
    kb.tt("dve", wuqf[:, :, :], wuqf[:, :, :], qgc[:, :].bc(2, [128, 2, 768]), ALU.mult)
    kb.cp("dve", wuq[:, :, :], wuqf[:, :, :])
    kb.memset("pool", wuqr[:, :, :], 0.0)
    wq4 = wuqf[:, :, :].re("p k (h c) -> p k h c", c=96)
    wr4 = wuqr[:, :, :].re("p k (h c) -> p k h c", c=96)
    kb.ts("pool", wr4[:, :, :, 64:80], wq4[:, :, :, 80:96], -1.0, None, ALU.mult)
    kb.cp("pool", wr4[:, :, :, 80:96], wq4[:, :, :, 64:80])
    kb.ts("dve", wukv[:, :], wukvf[:, :], kvgc[:, 0:1], None, ALU.mult)
    kb.pop()
    kb.push()
    psP = Ring([T(kb, f"psP{i}", [128, 512], F32, "psum") for i in range(4)])

    LAM = T(kb, "LAM", [64, 4, 32], F32)
    LDT = T(kb, "LDT", [64, 2, 32], F32)
    BT = T(kb, "BT", [64, 2, 32, 16], F32)
    CTt = T(kb, "CTt", [64, 4, 32, 16], F32)
    KVt = T(kb, "KVt", [64, 32, 40], F32)
    DTL = T(kb, "DTL", [128, 32], F32)
    MKF = T(kb, "MKF", [128, 128], F32)
    MKB = T(kb, "MKB", [128, 128], F32)
    kb.dma(LAM[:, :, :], lam[:, :, :])
    kb.dma(LDT[:, :, :], ldt[:, :, :])
    kb.dma(BT[:, :, :, :], bT[:, :, :, :])
    kb.dma(CTt[:, :, :, :], cT[:, :, :, :])
    kb.dma(KVt[:, :, :], kvc[:, :, :])
    kb.dma(DTL[:, :], dtile[:, :])
    kb.dma(MKF[:, :], maskF[:, :])
    kb.dma(MKB[:, :], maskB[:, :])
    PW = [T(kb, f"PW{d}", [64, 2, 32, 40], F32) for d in range(2)]
    BZ = [T(kb, f"BZ{d}", [64, 2, 32, 16], F32) for d in range(2)]
    kb.push()
    sc = [T(kb, f"sc{i}", [64, 32, 40], F32) for i in range(6)]
    sci = T(kb, "sci", [64, 32, 40], I32)
    sm = [T(kb, f"sm{i}", [64, 32], F32) for i in range(10)]
    smi = T(kb, "smi", [64, 32], I32)
    tmpb = T(kb, "tmpb", [64, 32, 16], F32)
    for d in range(2):
        DTt, RD, TH = sm[0], sm[1], sm[2]
        kb.act(DTt[:, :], LDT[:, d, :], AF.Exp)
        kb.tt("dve", RD[:, :], LAM[:, 2 * d, :], DTt[:, :], ALU.mult)
        kb.tt("dve", TH[:, :], LAM[:, 2 * d + 1, :], DTt[:, :], ALU.mult)
        ANG, MAG, SN, CS = sc[0], sc[1], sc[2], sc[3]
        kb.tt("dve", ANG[:, :, :], KVt[:, :, :], TH[:, :].bc(2, [64, 32, 40]), ALU.mult)
        kb.tt("dve", MAG[:, :, :], KVt[:, :, :], RD[:, :].bc(2, [64, 32, 40]), ALU.mult)
        kb.act(MAG[:, :, :], MAG[:, :, :], AF.Exp)
        sincos(kb, "pool", ANG[:, :, :], SN[:, :, :], CS[:, :, :], sc[4][:, :, :], sci[:, :, :], sc[5][:, :, :],
               sc[0][:, :, :])
        kb.tt("dve", PW[d][:, 0, :, :], MAG[:, :, :], CS[:, :, :], ALU.mult)
        kb.tt("dve", PW[d][:, 1, :, :], MAG[:, :, :], SN[:, :, :], ALU.mult)
        kb.act(rho8[:, d, :], RD[:, :], AF.Exp, scale=8.0)
        A8 = sm[3]
        kb.ts("dve", A8[:, :], TH[:, :], 8.0, None, ALU.mult)
        sincos(kb, "pool", A8[:, :], None, None, sm[4][:, :], smi[:, :], sm[5][:, :], None, r_out=phi8[:, d, :])
        nre, d2, t1, t2, zre, zim = sm[4], sm[5], sm[6], sm[7], sm[8], sm[9]
        are, aim = LAM[:, 2 * d, :], LAM[:, 2 * d + 1, :]
        lbre, lbim = PW[d][:, 0, :, 16], PW[d][:, 1, :, 16]
        kb.ts("dve", nre[:, :], lbre, -1.0, None, ALU.add)
        kb.tt("dve", d2[:, :], are, are, ALU.mult)
        kb.tt("dve", t1[:, :], aim, aim, ALU.mult)
        kb.tt("dve", d2[:, :], d2[:, :], t1[:, :], ALU.add)
        kb.recip(d2[:, :], d2[:, :])
        kb.tt("dve", t1[:, :], nre[:, :], are, ALU.mult)
        kb.tt("dve", t2[:, :], lbim, aim, ALU.mult)
        kb.tt("dve", t1[:, :], t1[:, :], t2[:, :], ALU.add)
        kb.tt("dve", zre[:, :], t1[:, :], d2[:, :], ALU.mult)
        kb.tt("dve", t1[:, :], lbim, are, ALU.mult)
        kb.tt("dve", t2[:, :], nre[:, :], aim, ALU.mult)
        kb.tt("dve", t1[:, :], t1[:, :], t2[:, :], ALU.subtract)
        kb.tt("dve", zim[:, :], t1[:, :], d2[:, :], ALU.mult)
        zreb, zimb = zre[:, :].bc(2, [64, 32, 16]), zim[:, :].bc(2, [64, 32, 16])
        kb.tt("dve", BZ[d][:, 0, :, :], BT[:, 0, :, :], zreb, ALU.mult)
        kb.tt("dve", tmpb[:, :, :], BT[:, 1, :, :], zimb, ALU.mult)
        kb.tt("dve", BZ[d][:, 0, :, :], BZ[d][:, 0, :, :], tmpb[:, :, :], ALU.subtract)
        kb.tt("dve", BZ[d][:, 1, :, :], BT[:, 0, :, :], zimb, ALU.mult)
        kb.tt("dve", tmpb[:, :, :], BT[:, 1, :, :], zreb, ALU.mult)
        kb.tt("dve", BZ[d][:, 1, :, :], BZ[d][:, 1, :, :], tmpb[:, :, :], ALU.add)

    kb.pop()
    LAMp = T(kb, "LAMp", [128, 4, 16], F32)
    LDTp = T(kb, "LDTp", [128, 2, 16], F32)
    kb.dma(LAMp[:, :, :], lamp[:, :, :])
    kb.dma(LDTp[:, :, :], ldtp[:, :, :])
    pp = [T(kb, f"pp{i}", [128, 16], F32) for i in range(6)]
    ppi = T(kb, "ppi", [128, 16], I32)
    for d in range(2):
        kb.act(pp[0][:, :], LDTp[:, d, :], AF.Exp)
        kb.tt("dve", pp[1][:, :], LAMp[:, 2 * d, :], pp[0][:, :], ALU.mult)
        kb.tt("dve", pp[2][:, :], LAMp[:, 2 * d + 1, :], pp[0][:, :], ALU.mult)
        kb.act(rho8p[:, d, :], pp[1][:, :], AF.Exp, scale=8.0)
        kb.ts("dve", pp[3][:, :], pp[2][:, :], 8.0, None, ALU.mult)
        sincos(kb, "pool", pp[3][:, :], None, None, pp[4][:, :], ppi[:, :], pp[5][:, :], None, r_out=phi8p[:, d, :])
    R1 = T(kb, "R1", [64, 2, 32, 8, 16], F32)
    R2 = T(kb, "R2", [64, 2, 32, 8, 16], F32)
    TMP = T(kb, "TMP", [64, 32, 8, 16], F32)
    SH = [64, 32, 8, 16]

    def cout(dst, pw, k0, yre, yim, sign):
        xre = pw[:, 0, :, k0:k0 + 8].bc(3, SH)
        xim = pw[:, 1, :, k0:k0 + 8].bc(3, SH)
        yreb, yimb = yre.bc(2, SH), yim.bc(2, SH)
        kb.tt("dve", dst[:, 0, :, :, :], xre, yreb, ALU.mult)
        kb.tt("pool", TMP[:, :, :, :], xim, yimb, ALU.mult)
        kb.tt("dve", dst[:, 0, :, :, :], dst[:, 0, :, :, :], TMP[:, :, :, :], ALU.subtract)
        kb.tt("dve", dst[:, 1, :, :, :], xre, yimb, ALU.mult)
        kb.tt("pool", TMP[:, :, :, :], xim, yreb, ALU.mult)
        kb.tt("dve", dst[:, 1, :, :, :], dst[:, 1, :, :, :], TMP[:, :, :, :], ALU.add)
        if sign < 0:
            kb.ts("pool", dst[:, 1, :, :, :], dst[:, 1, :, :, :], -1.0, None, ALU.mult)

    wsbR = Ring([T(kb, f"wsb{i}", [128, 4, 2, 64], BF16) for i in range(2)])

    def ws_transposes(src, d):
        for g0 in range(0, 32, 4):
            ps = psP.next()
            for gi in range(4):
                for ri in range(2):
                    kb.tr(ps[:, (gi * 2 + ri) * 64:(gi * 2 + ri + 1) * 64],
                          src[:, ri, g0 + gi, :, :].re("p j q -> p (j q)"), ident[0:64, 0:64])
            wsb = wsbR.next()
            kb.cp("act", wsb[:, :, :, :].re("p g r n -> p (g r n)"), ps[:, :])
            kb.dma(WSD[:, g0:g0 + 4, d, :, :].s(), wsb[:, :, :, :])

    TOE = T(kb, "TOE", [128, 32, 128], F32)
    TOT = T(kb, "TOT", [128, 4, 128], F32)

    def toep(Asrc, Csrc, mask, first):
        for g0 in range(0, 32, 4):
            ps = psP.next()
            for gi in range(4):
                g = g0 + gi
                o = ps[:, gi * 128:(gi + 1) * 128]
                kb.mm(o, Asrc[:, 0, g, :, :].re("p j q -> p (j q)"), Csrc[:, 0, g, :, :].re("p j q -> p (j q)"),
                      start=True, stop=False)
                kb.mm(o, Asrc[:, 1, g, :, :].re("p j q -> p (j q)"), Csrc[:, 1, g, :, :].re("p j q -> p (j q)"),
                      start=False, stop=True)
            mb4 = mask[:, :].bc(1, [128, 4, 128])
            if first:
                kb.tt("dve", TOE[:, g0:g0 + 4, :], ps[:, :].re("p (g c) -> p g c", g=4), mb4, ALU.mult)
            else:
                kb.tt("dve", TOT[:, :, :], ps[:, :].re("p (g c) -> p g c", g=4), mb4, ALU.mult)
                kb.tt("pool", TOE[:, g0:g0 + 4, :], TOE[:, g0:g0 + 4, :], TOT[:, :, :], ALU.add)

    cout(R1, PW[0], 0, BZ[0][:, 0, :, :], BZ[0][:, 1, :, :], +1)
    ws_transposes(R1, 0)
    cout(R1, PW[1], 8, BZ[1][:, 0, :, :], BZ[1][:, 1, :, :], +1)
    ws_transposes(R1, 1)
    cout(R2, PW[1], 32, CTt[:, 2, :, :], CTt[:, 3, :, :], -1)
    toep(R1, R2, MKB, True)
    cout(R1, PW[0], 32, BZ[0][:, 0, :, :], BZ[0][:, 1, :, :], +1)
    cout(R2, PW[0], 8, CTt[:, 0, :, :], CTt[:, 1, :, :], -1)
    toep(R1, R2, MKF, False)
    TOEb = Ring([T(kb, f"TOEb{i}", [128, 4, 128], BF16) for i in range(2)])
    for g0 in range(0, 32, 4):
        kb.tt("dve", TOT[:, :, :], ident[:, :].bc(1, [128, 4, 128]), DTL[:, g0:g0 + 4].bc(2, [128, 4, 128]), ALU.mult)
        kb.tt("dve", TOE[:, g0:g0 + 4, :], TOE[:, g0:g0 + 4, :], TOT[:, :, :], ALU.add)
        tb = TOEb.next()
        kb.cp("act", tb[:, :, :], TOE[:, g0:g0 + 4, :])
        kb.dma(TOED[:, g0:g0 + 4, :].s(), tb[:, :, :])
    CLb = Ring([T(kb, f"CLb{i}", [64, 32, 128], BF16) for i in range(2)])
    cout(R1, PW[0], 16, CTt[:, 0, :, :], CTt[:, 1, :, :], -1)
    cout(R2, PW[1], 24, CTt[:, 2, :, :], CTt[:, 3, :, :], -1)
    for d, R in ((0, R1), (1, R2)):
        for ri in range(2):
            cb_ = CLb.next()
            kb.cp("act" if ri == 0 else "dve", cb_[:, :, :], R[:, ri, :, :, :].re("p g j q -> p g (j q)"))
            kb.dma(CLD[:, :, d, ri, :].s(), cb_[:, :, :])
    kb.pop()

    kb.push()
    NBmax = max(cfg.NB)
    ntmax = NBmax // 128
    WA = T(kb, "WA", [128, 8, 992], BF16)
    kb.dma(WA[:, :, 0:512].s(), WD[:, :, 0:512])
    kb.dma(WA[:, :, 512:768].s(), WD[:, :, 1024:1280])
    kb.dma(WA[:, :, 768:896].s(), WD[:, :, 1280:1408])
    kb.dma(WA[:, :, 896:928].s(), WD[:, :, 1408:1440])
    kb.dma(WA[:, :, 928:960].s(), WD[:, :, 4000:4032])
    xinR = Ring([T(kb, f"xin{i}", [128, ntmax, 1024], F32) for i in range(2)])
    xcR = Ring([T(kb, f"xc{i}", [128, ntmax, 1024], BF16) for i in range(2)])
    xcTR = Ring([T(kb, f"xcT{i}", [128, 8, NBmax], BF16) for i in range(2)])
    stR = Ring([T(kb, f"st{i}", [128, ntmax, 2, 6], F32) for i in range(2)])
    mvR = Ring([T(kb, f"mv{i}", [128, ntmax, 2], F32) for i in range(2)])
    sdR = Ring([T(kb, f"sd{i}", [128, ntmax], F32) for i in range(2)])
    pTR = Ring([T(kb, f"pT{i}", [128, 8, 128], BF16, "psum") for i in range(2)])
    pmR = Ring([T(kb, f"pm{i}", [128, 512], F32, "psum") for i in range(6)])
    ubR = Ring([T(kb, f"ub{i}", [128, 4, NBmax], BF16) for i in range(2)])
    posR = Ring([T(kb, f"posb{i}", [128, NBmax], F32) for i in range(2)])
    rtf = [T(kb, f"rtf{i}", [128, NBmax], F32) for i in range(4)]
    rti = T(kb, "rti", [128, NBmax], I32)
    cosR = Ring([T(kb, f"cosT{i}", [128, NBmax], F32) for i in range(2)])
    sinR = Ring([T(kb, f"sinT{i}", [128, NBmax], F32) for i in range(2)])
    ckvf = T(kb, "ckvf", [128, NBmax], F32)
    sqf = T(kb, "sqf", [128, NBmax], F32)
    rsf = T(kb, "rsf", [128, NBmax], F32)
    ckvbR = Ring([T(kb, f"ckvb{i}", [128, NBmax], BF16) for i in range(2)])
    krt = [T(kb, f"krt{i}", [32, NBmax], F32) for i in range(2)]
    krbR = Ring([T(kb, f"krb{i}", [32, NBmax], BF16) for i in range(2)])
    cqf = T(kb, "cqf", [128, 2, NBmax], F32)
    sq2 = T(kb, "sq2", [128, 2, NBmax], F32)
    cqb = T(kb, "cqb", [128, 2, NBmax], BF16)
    qtA = Ring([T(kb, f"qtA{i}", [96, NBmax], F32) for i in range(2)])
    qtB = Ring([T(kb, f"qtB{i}", [96, NBmax], F32) for i in range(2)])
    rsq = T(kb, "rsq", [128, NBmax], F32)
    qbR = Ring([T(kb, f"qb{i}", [96, NBmax], BF16) for i in range(3)])

    def ln_load(xsrc, NB):
        nt = NB // 128
        xin = xinR.next()
        kb.dma(xin[:, 0:nt, :], xsrc)
        return xin

    def ln_norm(xin, NB, want_f32=None):
        nt = NB // 128
        xc, st, mv, sd = xcR.next(), stR.next(), mvR.next(), sdR.next()
        for ti in range(nt):
            for hh in range(2):
                kb.bn_stats(st[:, ti, hh, :].s(), xin[:, ti, hh * 512:(hh + 1) * 512])
            kb.bn_aggr(mv[:, ti, :].s(), st[:, ti, :, :].re("p a b -> p (a b)"))
        kb.act(sd[:, 0:nt], mv[:, 0:nt, 1], AF.Ln, bias=epsl[:, 0:1])
        kb.act(sd[:, 0:nt], sd[:, 0:nt], AF.Exp, scale=-0.5)
        kb.ts("dve", mv[:, 0:nt, 0], mv[:, 0:nt, 0], -1.0, None, ALU.mult)
        for ti in range(nt):
            eng = "dve" if ti % 2 == 0 else "pool"
            kb.ts(eng, xc[:, ti, :].s(), xin[:, ti, :], mv[:, ti, 0:1], sd[:, ti:ti + 1], ALU.add, ALU.mult)
            if want_f32 is not None:
                kb.ts("pool", want_f32[:, ti, :].s(), xin[:, ti, :], mv[:, ti, 0:1], sd[:, ti:ti + 1], ALU.add,
                      ALU.mult)
        return xc

    def ln_tr(xc, NB):
        nt = NB // 128
        xcT = xcTR.next()
        for ti in range(nt):
            pT = pTR.next()
            for k in range(8):
                kb.tr(pT[:, k, :], xc[:, ti, k * 128:(k + 1) * 128], identb[:, :])
            kb.cp("act" if ti % 2 == 0 else "dve", xcT[:, :, ti * 128:(ti + 1) * 128].s(), pT[:, :, :])
        return xcT

    def proj(xcT, wt, wc0, width, NB):
        ps = pmR.next()
        for k in range(8):
            kb.mm(ps[0:width, 0:NB], wt[:, k, wc0:wc0 + width], xcT[:, k, 0:NB], start=(k == 0), stop=(k == 7))
        return ps

    blocksA = [(s, j, cb) for s in range(3) for j in range(8) for cb in range(cfg.nC[s] // cfg.NB[s])]
    ctxA = {}

    def A_load(i):
        s, j, cb = blocksA[i]
        NB, nC = cfg.NB[s], cfg.nC[s]
        r0 = j * nC + cb * NB
        xsrc = X(xr[s].a[r0:r0 + NB, :].rearrange("(t p) d -> p t d", p=128), xr[s].b)
        xin = ln_load(xsrc, NB)
        posb = posR.next()
        kb.dma(posb[:, 0:NB], X(pos[s].a[0:1, r0:r0 + NB].partition_broadcast(128), pos[s].b))
        ctxA[i] = dict(xin=xin, posb=posb)

    def A_norm(i):
        s, j, cb = blocksA[i]
        NB = cfg.NB[s]
        c = ctxA[i]
        c["xc"] = ln_norm(c["xin"], NB)

    def A_rope(i):
        s, j, cb = blocksA[i]
        NB = cfg.NB[s]
        c = ctxA[i]
        posb, cosT, sinT = c["posb"], cosR.next(), sinR.next()
        kb.ts("dve", rtf[0][:, 0:NB], posb[:, 0:NB], invc[:, 0:1], None, ALU.mult)
        sincos(kb, "pool", rtf[0][:, 0:NB], sinT[:, 0:NB], cosT[:, 0:NB], rtf[1][:, 0:NB], rti[:, 0:NB],
               rtf[2][:, 0:NB], rtf[3][:, 0:NB])
        c["cosT"], c["sinT"] = cosT, sinT

    def A_n2(i):
        s, j, cb = blocksA[i]
        c = ctxA[i]
        c["xcT"] = ln_tr(c["xc"], cfg.NB[s])

    def A_main(i):
        s, j, cb = blocksA[i]
        NB, nC, nCo = cfg.NB[s], cfg.nC[s], cfg.nCo[s]
        r0 = j * nC + cb * NB
        c0 = cb * NB
        own = (cb == 0)
        c = ctxA.pop(i)
        xcT, cosT, sinT = c["xcT"], c["cosT"], c["sinT"]
        ub = ubR.next()
        for mt in range(4):
            ps = proj(xcT, WA, mt * 128, 128, NB)
            kb.act(ub[:, mt, 0:NB].s(), ps[:, 0:NB], AF.Identity, bias=hbT[:, SLOT[("u", mt)]:SLOT[("u", mt)] + 1])
        ps = proj(xcT, WA, 768, 128, NB)
        sl = SLOT[("ckv", 0)]
        kb.act(ckvf[:, 0:NB], ps[:, 0:NB], AF.Identity, bias=hbT[:, sl:sl + 1])
        kb.tt("pool", sqf[:, 0:NB], ckvf[:, 0:NB], ckvf[:, 0:NB], ALU.mult)
        ps = proj(xcT, WA, 896, 32, NB)
        psr = proj(xcT, WA, 928, 32, NB)
        sl, slr = SLOT[("kr", 0)], SLOT[("krr", 0)]
        kb.act(krt[0][:, 0:NB], ps[0:32, 0:NB], AF.Identity, bias=hbT[0:32, sl:sl + 1])
        kb.act(krt[1][:, 0:NB], psr[0:32, 0:NB], AF.Identity, bias=hbT[0:32, slr:slr + 1])
        if own:
            for k2 in range(2):
                ps = proj(xcT, WA, 512 + k2 * 128, 128, NB)
                sl = SLOT[("cq", k2)]
                kb.act(cqf[:, k2, 0:NB].s(), ps[:, 0:NB], AF.Identity, bias=hbT[:, sl:sl + 1])
            kb.tt("pool", sq2[:, :, 0:NB], cqf[:, :, 0:NB], cqf[:, :, 0:NB], ALU.mult)
        for mt in range(4):
            kb.dma(uD[s][mt, j, :, c0:c0 + NB].s(), ub[:, mt, 0:NB])
        if i + 1 < len(blocksA):
            A_rope(i + 1)
        ps2 = pmR.next()
        kb.mm(ps2[:, 0:NB], ones_f[:, :], sqf[:, 0:NB])
        kb.act(rsf[:, 0:NB], ps2[:, 0:NB], AF.Ln, bias=epsr[:, 0:1], scale=1.0 / 128)
        if own:
            ps3 = pmR.next()
            for k2 in range(2):
                kb.mm(ps3[:, 0:NB], ones_f[:, :], sq2[:, k2, 0:NB], start=(k2 == 0), stop=(k2 == 1))
            kb.act(rsq[:, 0:NB], ps3[:, 0:NB], AF.Ln, bias=epsr[:, 0:1], scale=1.0 / 256)
            kb.act(rsq[:, 0:NB], rsq[:, 0:NB], AF.Exp, scale=-0.5)
        kb.act(rsf[:, 0:NB], rsf[:, 0:NB], AF.Exp, scale=-0.5)
        ckvb = ckvbR.next()
        kb.tt("dve", ckvb[:, 0:NB], ckvf[:, 0:NB], rsf[:, 0:NB], ALU.mult)
        kb.dma(kvD[s][:, r0:r0 + NB].s(), ckvb[:, 0:NB])
        kb.tt("dve", krt[0][:, 0:NB], krt[0][:, 0:NB], cosT[0:32, 0:NB], ALU.mult)
        kb.tt("pool", krt[1][:, 0:NB], krt[1][:, 0:NB], sinT[0:32, 0:NB], ALU.mult)
        krb = krbR.next()
        kb.tt("dve", krb[:, 0:NB], krt[0][:, 0:NB], krt[1][:, 0:NB], ALU.add)
        kb.dma(krD[s][:, r0:r0 + NB].s(), krb[:, 0:NB])
        if own:
            oc0 = j * nCo
            kb.tt("dve", cqb[:, :, 0:NB], cqf[:, :, 0:NB], rsq[:, 0:NB].bc(1, [128, 2, NB]), ALU.mult)
            for h in range(8):
                psq, psqr = pmR.next(), pmR.next()
                for k2 in range(2):
                    kb.mm(psq[0:96, 0:NB], wuq[:, k2, h * 96:(h + 1) * 96], cqb[:, k2, 0:NB],
                          start=(k2 == 0), stop=(k2 == 1))
                for k2 in range(2):
                    kb.mm(psqr[0:96, 0:NB], wuqr[:, k2, h * 96:(h + 1) * 96], cqb[:, k2, 0:NB],
                          start=(k2 == 0), stop=(k2 == 1))
                qb = qbR.next()
                qa, qc = qtA.next(), qtB.next()
                kb.cp("act", qb[0:64, 0:NB].s(), psq[0:64, 0:NB])
                kb.tt("dve", qa[64:96, 0:NB], psq[64:96, 0:NB], cosT[64:96, 0:NB], ALU.mult)
                kb.tt("dve", qc[64:96, 0:NB], psqr[64:96, 0:NB], sinT[64:96, 0:NB], ALU.mult)
                kb.tt("pool", qb[64:96, 0:NB].s(), qa[64:96, 0:NB], qc[64:96, 0:NB], ALU.add)
                kb.dma(qD[s][h, :, oc0:oc0 + NB].s(), qb[:, 0:NB])

    nA = len(blocksA)
    A_load(0)
    if nA > 1:
        A_load(1)
    A_norm(0)
    A_rope(0)
    A_n2(0)
    for i in range(nA):
        if i + 2 < nA:
            A_load(i + 2)
        if i + 1 < nA:
            A_norm(i + 1)
        A_main(i)
        if i + 1 < nA:
            A_n2(i + 1)
    kb.pop()
    if cfg.debug == "A":
        kb.close()
        return nc

    kb.push()
    nCm = max(cfg.nC)
    nCom = max(cfg.nCo)
    WSr = Ring([T(kb, f"WSr{i}", [128, 2, 2, 2, 64], BF16) for i in range(2)])
    TOEr = Ring([T(kb, f"TOEr{i}", [128, 2, 128], BF16) for i in range(2)])
    CLr = Ring([T(kb, f"CLr{i}", [128, 2, 2, 2, 128], BF16) for i in range(2)])
    for cl in CLr.tiles:
        kb.memset("pool", cl[:, :, :, :, :], 0.0)
    IOT = T(kb, "IOT", [128, nCm], F32)
    kb.dma(IOT[0:64, :].s(), iotaf[:, 0:nCm])
    kb.dma(IOT[64:128, :].s(), iotaf[:, 0:nCm])
    MF = [T(kb, f"MF{s}", [128, cfg.nC[s]], BF16) for s in range(3)]
    MB = [T(kb, f"MB{s}", [128, cfg.nC[s]], BF16) for s in range(3)]
    for s in range(3):
        kb.dma(MF[s][:, :], X(mfm[s].a[0:1, :].partition_broadcast(128), mfm[s].b), q="pool")
        kb.dma(MB[s][:, :], X(mbm[s].a[0:1, :].partition_broadcast(128), mbm[s].b), q="pool")
    uchR = [[Ring([T(kb, f"uch{s}_{gi}_{i}", [128, cfg.nC[s]], BF16) for i in range(2)])
             for gi in range(2)] for s in range(3)]
    CTr = Ring([T(kb, f"CTb{i}", [128, nCm], F32) for i in range(2)])
    STr = Ring([T(kb, f"STb{i}", [128, nCm], F32) for i in range(2)])
    tabS = {}

    def S_tables(it):
        k_, d_ = divmod(it, 2)
        CTb, STb = CTr.next(), STr.next()
        kb.act(tg[0][:, :], IOT[:, :], AF.Identity, scale=phi8p[:, d_, k_:k_ + 1])
        sincos(kb, "act", tg[0][:, :], STb[:, :], CTb[:, :], tg[1][:, :], tgi[:, :], tg[0][:, :], tg[1][:, :])
        tabS[it] = (CTb, STb)

    tg = [T(kb, f"tg{i}", [128, nCm], F32) for i in range(2)]
    tgi = T(kb, "tgi", [128, nCm], I32)
    fre = T(kb, "Fre", [128, nCm], F32)
    fim = T(kb, "Fim", [128, nCm], F32)
    wre = T(kb, "Wre", [128, nCm], F32)
    wim = T(kb, "Wim", [128, nCm], F32)
    a0 = T(kb, "A0", [128, nCm], F32)
    rt_ = [T(kb, f"rt{i}", [128, 512], F32) for i in range(8)]
    ut_ = [T(kb, f"ut{i}", [128, nCom], F32) for i in range(4)]
    HH = [[Ring([T(kb, f"H{s}_{d}_{i}", [128, 2, cfg.nCo[s]], BF16) for i in range(1)]) for d in range(2)]
          for s in range(3)]
    psS = Ring([T(kb, f"psS{i}", [128, 512], F32, "psum") for i in range(4)])
    psY = Ring([T(kb, f"psY{i}", [128, 512], F32, "psum") for i in range(2)])
    gl = ut_[0:3]
    ygR = Ring([T(kb, f"yg{i}", [128, nCom], BF16) for i in range(2)])
    ctxS = {}

    def S_load(k):
        WSg, TOEg, CLg = WSr.next(), TOEr.next(), CLr.next()
        kb.dma(WSg[:, :, :, :, :], WSD[:, 2 * k:2 * k + 2, :, :, :])
        kb.dma(TOEg[:, :, :], TOED[:, 2 * k:2 * k + 2, :])
        for gi in range(2):
            kb.dma(CLg[64 * gi:64 * gi + 64, gi, :, :, :].s(), CLD[:, 2 * k + gi, :, :, :])
        uch = []
        for s in range(3):
            nC, nCo = cfg.nC[s], cfg.nCo[s]
            us = []
            for gi in range(2):
                g = 2 * k + gi
                u = uchR[s][gi].next()
                for jj in range(8):
                    kb.dma(u[16 * jj:16 * jj + 16, 0:nC].s(), uD[s][g // 8, jj, 16 * (g % 8):16 * (g % 8) + 16, :])
                us.append(u)
            uch.append(us)
        ctxS[k] = (WSg, TOEg, CLg, uch)

    def S_compute(k):
        WSg, TOEg, CLg, uch = ctxS.pop(k)
        Hcur = [[None, None] for _ in range(3)]
        for d in range(2):
            CTb, STb = tabS.pop(2 * k + d)
            for s in range(3):
                nC, nCo = cfg.nC[s], cfg.nCo[s]
                base = nCo if d == 0 else 0
                pend = None
                for pk_i, p0 in enumerate(range(0, nC, 512)):
                    w = min(512, nC - p0)
                    pr, pi = psS.next(), psS.next()
                    src0 = (p0 + base) % nC
                    segs = [(src0, 0, w)] if src0 + w <= nC else [(src0, 0, nC - src0), (0, nC - src0, w - (nC - src0))]
                    for gi in range(2):
                        u = uch[s][gi]
                        for (sc0, oc_, sw) in segs:
                            kb.mm(pr[64 * gi:64 * gi + 64, oc_:oc_ + sw], WSg[:, gi, d, 0, :], u[:, sc0:sc0 + sw])
                            kb.mm(pi[64 * gi:64 * gi + 64, oc_:oc_ + sw], WSg[:, gi, d, 1, :], u[:, sc0:sc0 + sw])
                    ct, st_ = CTb[:, p0:p0 + w], STb[:, p0:p0 + w]
                    r = rt_[(pk_i % 2) * 4:(pk_i % 2) * 4 + 4]
                    kb.tt("dve", r[0][:, 0:w], pr[:, 0:w], ct, ALU.mult)
                    kb.tt("dve", r[1][:, 0:w], pi[:, 0:w], st_, ALU.mult)
                    kb.tt("dve", r[2][:, 0:w], pi[:, 0:w], ct, ALU.mult)
                    kb.tt("dve", r[3][:, 0:w], pr[:, 0:w], st_, ALU.mult)
                    if pend is not None:
                        pr_, pp0, pw = pend
                        kb.tt("dve", fre[:, pp0:pp0 + pw].s(), pr_[0][:, 0:pw], pr_[1][:, 0:pw], ALU.add if d == 0 else ALU.subtract)
                        kb.tt("dve", fim[:, pp0:pp0 + pw].s(), pr_[2][:, 0:pw], pr_[3][:, 0:pw], ALU.subtract if d == 0 else ALU.add)
                    pend = (r, p0, w)
                pr_, pp0, pw = pend
                kb.tt("dve", fre[:, pp0:pp0 + pw].s(), pr_[0][:, 0:pw], pr_[1][:, 0:pw], ALU.add if d == 0 else ALU.subtract)
                kb.tt("dve", fim[:, pp0:pp0 + pw].s(), pr_[2][:, 0:pw], pr_[3][:, 0:pw], ALU.subtract if d == 0 else ALU.add)
                msk = MF[s] if d == 0 else MB[s]
                kb.act(a0[:, 0:nC], msk[:, :], AF.Identity, scale=rho8p[:, d, k:k + 1])
                if d == 0:
                    kb.scan(wre[:, 0:nC], a0[:, 0:nC], fre[:, 0:nC])
                    kb.scan(wim[:, 0:nC], a0[:, 0:nC], fim[:, 0:nC])
                    q0, m0 = nC - nCo - 1, nC - nCo
                else:
                    kb.scan(wre[:, 0:nC][:, ::-1], a0[:, 0:nC][:, ::-1], fre[:, 0:nC][:, ::-1])
                    kb.scan(wim[:, 0:nC][:, ::-1], a0[:, 0:nC][:, ::-1], fim[:, 0:nC][:, ::-1])
                    q0, m0 = 1, 0
                H = HH[s][d].next()
                ct, st_ = CTb[:, q0:q0 + nCo], STb[:, q0:q0 + nCo]
                wr, wi = wre[:, q0:q0 + nCo], wim[:, q0:q0 + nCo]
                mk = msk[:, m0:m0 + nCo]
                kb.tt("dve", ut_[0][:, 0:nCo], ct, wr, ALU.mult)
                kb.tt("dve", ut_[1][:, 0:nCo], st_, wi, ALU.mult)
                kb.tt("pool", ut_[2][:, 0:nCo], ct, wi, ALU.mult)
                kb.tt("pool", ut_[3][:, 0:nCo], st_, wr, ALU.mult)
                kb.tt("dve", ut_[0][:, 0:nCo], ut_[0][:, 0:nCo], ut_[1][:, 0:nCo], ALU.subtract if d == 0 else ALU.add)
                kb.tt("pool", ut_[2][:, 0:nCo], ut_[2][:, 0:nCo], ut_[3][:, 0:nCo], ALU.add if d == 0 else ALU.subtract)
                kb.tt("dve", H[:, 0, :].s(), ut_[0][:, 0:nCo], mk, ALU.mult)
                kb.tt("pool", H[:, 1, :].s(), ut_[2][:, 0:nCo], mk, ALU.mult)
                Hcur[s][d] = H
                if s == 0 and 2 * k + d + 1 < 32:
                    S_tables(2 * k + d + 1)
        for s in range(3):
            nC, nCo = cfg.nC[s], cfg.nCo[s]
            for gi in range(2):
                g = 2 * k + gi
                u = uch[s][gi]
                lo, hi = 64 * gi, 64 * gi + 64
                py = psY.next()
                kb.mm(py[:, 0:nCo], TOEg[:, gi, :], u[:, 0:nCo], start=True, stop=False)
                kb.mm(py[:, 0:nCo], CLg[:, gi, 0, 0, :], Hcur[s][0][:, 0, :], start=False, stop=False)
                kb.mm(py[:, 0:nCo], CLg[:, gi, 0, 1, :], Hcur[s][0][:, 1, :], start=False, stop=False)
                kb.mm(py[:, 0:nCo], CLg[:, gi, 1, 0, :], Hcur[s][1][:, 0, :], start=False, stop=False)
                kb.mm(py[:, 0:nCo], CLg[:, gi, 1, 1, :], Hcur[s][1][:, 1, :], start=False, stop=True)
                y = py[:, 0:nCo]
                kb.act(gl[0][:, 0:nCo], y, AF.Square)
                kb.ts("pool", gl[0][:, 0:nCo], gl[0][:, 0:nCo], 0.044715, 1.0, ALU.mult, ALU.add)
                kb.tt("dve", gl[1][:, 0:nCo], gl[0][:, 0:nCo], y, ALU.mult)
                kb.act(gl[2][:, 0:nCo], gl[1][:, 0:nCo], AF.Sigmoid, scale=GELU_K)
                yg = ygR.next()
                kb.tt("dve", yg[:, 0:nCo], gl[2][:, 0:nCo], y, ALU.mult)
                for tt_ in range(8):
                    kb.dma(yD[s][g // 8, tt_, 16 * (g % 8):16 * (g % 8) + 16, :].s(), yg[16 * tt_:16 * tt_ + 16, 0:nCo])

    S_load(0)
    S_tables(0)
    for k in range(16):
        if k + 1 < 16:
            S_load(k + 1)
        S_compute(k)
    kb.pop()
    if cfg.debug == "S":
        kb.close()
        return nc

    kb.push()
    Lm, Lom = max(cfg.L), max(cfg.Lo)
    QBm = min(512, Lom)
    ckvn = T(kb, "ckvn", [128, Lm], BF16)
    KTs = [T(kb, f"KT{i}", [96, Lm], BF16) for i in range(2)]
    Vts = [T(kb, f"Vt{i}", [128, Lm // 128, 65], BF16) for i in range(2)]
    QTs = [T(kb, f"QT{i}", [96, Lom], BF16) for i in range(2)]
    for v in Vts:
        kb.memset("pool", v[:, :, 64:65], 1.0)
    selt = T(kb, "selt", [128, 64], F32)
    kb.dma(selt[:, :], sel[:, :])
    rc = T(kb, "rc", [128, QBm], F32)
    kb.memset("dve", rc[:, :], 0.0)
    psSt = Ring([T(kb, f"psSt{i}", [128, 1024], F32, "psum") for i in range(3)])
    psO = Ring([T(kb, f"psO{i}", [128, 512], F32, "psum") for i in range(2)])
    PT = Ring([T(kb, f"PT{i}", [128, 2 * QBm], BF16) for i in range(3)])
    bcs = T(kb, "bcs", [64, QBm], F32)
    yab = Ring([T(kb, f"yab{i}", [64, QBm], BF16) for i in range(2)])
    DEPTH = 2
    for s in range(3):
        L, Lo = cfg.L[s], cfg.Lo[s]
        QB = min(512, Lo)
        nkt = L // 128
        nkp = nkt // 2
        kb.dma(ckvn[:, 0:L], kvD[s][:, :])
        for KT in KTs:
            kb.dma(KT[64:96, 0:L], krD[s][:, :])

        def build_kv(h):
            KT, V = KTs[h % 2], Vts[h % 2]
            for p0 in range(0, L, 1024):
                pk = psSt.next()
                for e in range(2):
                    kb.mm(pk[0:64, e * 512:(e + 1) * 512], wukv[:, h * 128:h * 128 + 64],
                          ckvn[:, p0 + e * 512:p0 + (e + 1) * 512])
                kb.cp("dve" if (p0 // 1024) % 2 == 0 else "act", KT[0:64, p0:p0 + 1024].s(), pk[0:64, :])
            for kt0 in range(0, nkt, 16):
                pk = psSt.next()
                for i in range(16):
                    kb.mm(pk[:, i * 64:(i + 1) * 64], ckvn[:, (kt0 + i) * 128:(kt0 + i + 1) * 128],
                          wukv[:, h * 128 + 64:h * 128 + 128])
                kb.cp("act" if (kt0 // 16) % 2 == 0 else "dve", V[:, kt0:kt0 + 16, 0:64].s(),
                      pk[:, :].re("p (i c) -> p i c", c=64))
            kb.dma(QTs[h % 2][:, 0:Lo], qD[s][h, :, :])

        build_kv(0)
        for h in range(8):
            if h + 1 < 8:
                build_kv(h + 1)
            KT, V, Q = KTs[h % 2], Vts[h % 2], QTs[h % 2]
            units = [(qb0, kp) for qb0 in range(0, Lo, QB) for kp in range(nkp)]
            pst_of = {}
            po_of = {}
            pending = []

            def qk(i):
                qb0, kp = units[i]
                pst = psSt.next()
                for e in range(2):
                    kt = 2 * kp + e
                    kb.mm(pst[:, e * 512:e * 512 + QB], KT[0:96, kt * 128:(kt + 1) * 128], Q[0:96, qb0:qb0 + QB])
                pst_of[i] = pst

            def fin_pe(qb0, po):
                kb.recip(rc[64:65, 0:QB], po[64:65, 0:QB])
                psB = psSt.next()
                kb.mm(psB[0:64, 0:QB], selt[:, :], rc[:, 0:QB])
                kb.cp("dve", bcs[:, 0:QB], psB[0:64, 0:QB])
                ya = yab.next()
                kb.tt("dve", ya[:, 0:QB], po[0:64, 0:QB], bcs[:, 0:QB], ALU.mult)
                kb.dma(yaD[s][h, :, qb0:qb0 + QB].s(), ya[:, 0:QB])

            for i in range(min(DEPTH, len(units))):
                qk(i)
            for i, (qb0, kp) in enumerate(units):
                if kp == 0:
                    po_of[qb0] = psO.next()
                po = po_of[qb0]
                pst = pst_of.pop(i)
                pt = PT.next()
                if QB == 512:
                    kb.act(pt[:, 0:1024], pst[:, 0:1024], AF.Exp, scale=SCALE)
                else:
                    for e in range(2):
                        kb.act(pt[:, e * 512:e * 512 + QB], pst[:, e * 512:e * 512 + QB], AF.Exp, scale=SCALE)
                if i + DEPTH < len(units):
                    qk(i + DEPTH)
                for e in range(2):
                    kt = 2 * kp + e
                    kb.mm(po[0:65, 0:QB], V[:, kt, 0:65], pt[:, e * 512:e * 512 + QB], start=(kt == 0),
                          stop=(kt == nkt - 1))
                if pending and pending[0][0] <= i:
                    _, fq, fpo = pending.pop(0)
                    fin_pe(fq, fpo)
                if kp == nkp - 1:
                    pending.append((i + 2, qb0, po))
            for _, fq, fpo in pending:
                fin_pe(fq, fpo)
    kb.pop()
    if cfg.debug == "B":
        kb.close()
        return nc

    kb.push()
    WC = T(kb, "WC", [128, 8, 3072], BF16)
    kb.dma(WC[:, :, 0:512].s(), WD[:, :, 512:1024])
    kb.dma(WC[:, :, 512:1024].s(), WD[:, :, 1440:1952])
    kb.dma(WC[:, :, 1024:2048].s(), WD[:, :, 1952:2976])
    kb.dma(WC[:, :, 2048:3072].s(), WD[:, :, 2976:4000])
    wglu = T(kb, "wglu", [128, 4, 512], BF16)
    wbs = T(kb, "wbs", [128, 4, 1024], BF16)
    wba = T(kb, "wba", [128, 4, 1024], BF16)
    wo = T(kb, "wo", [128, 8, 1024], BF16)
    kb.dma(wglu[:, :, :], X(w_glu.a.rearrange("(k p) c -> p k c", p=128), w_glu.b), q="pool")
    kb.dma(wbs[:, :, :], X(w_bs.a.rearrange("(k p) c -> p k c", p=128), w_bs.b), q="pool")
    kb.dma(wba[:, :, :], X(w_ba.a.rearrange("(k p) c -> p k c", p=128), w_ba.b), q="pool")
    kb.dma(wo[:, :, :], X(w_o.a.rearrange("(k p) c -> p k c", p=128), w_o.b), q="pool")
    agb = T(kb, "agb", [128, 1024], F32)
    abb = T(kb, "abb", [128, 1024], F32)
    ogb = T(kb, "ogb", [128, 1024], F32)
    obb = T(kb, "obb", [128, 1024], F32)
    kb.dma(agb[:, :], X(lng_row.a[0:1, :].partition_broadcast(128), lng_row.b))
    kb.dma(abb[:, :], X(lnb_row.a[0:1, :].partition_broadcast(128), lnb_row.b))
    kb.dma(ogb[:, :], X(lno_g.a[0:1, :].partition_broadcast(128), lno_g.b))
    kb.dma(obb[:, :], X(lno_b.a[0:1, :].partition_broadcast(128), lno_b.b))
    kb.ts("dve", agb[:, :], agb[:, :], ALPHA, None, ALU.mult)
    kb.ts("dve", abb[:, :], abb[:, :], ALPHA, None, ALU.mult)
    NBc = min(256, NBmax)
    ntc = NBc // 128
    xinR = Ring([T(kb, f"xinC{i}", [128, ntc, 1024], F32) for i in range(2)])
    xcR = Ring([T(kb, f"xcC{i}", [128, ntc, 1024], BF16) for i in range(2)])
    xcTR = Ring([T(kb, f"xcTC{i}", [128, 8, NBc], BF16) for i in range(2)])
    stR = Ring([T(kb, f"stC{i}", [128, ntc, 2, 6], F32) for i in range(2)])
    mvR = Ring([T(kb, f"mvC{i}", [128, ntc, 2], F32) for i in range(2)])
    sdR = Ring([T(kb, f"sdC{i}", [128, ntc], F32) for i in range(2)])
    pTR = Ring([T(kb, f"pTC{i}", [128, 8, 128], BF16, "psum") for i in range(2)])
    pmR = Ring([T(kb, f"pmC{i}", [128, 512], F32, "psum") for i in range(6)])
    resR = Ring([T(kb, f"res{i}", [128, ntc, 1024], F32) for i in range(2)])
    ysR = Ring([T(kb, f"ys{i}", [128, 4, NBc], BF16) for i in range(3)])
    yaR = Ring([T(kb, f"yaC{i}", [128, 4, NBc], BF16) for i in range(3)])
    zv = T(kb, "zv", [128, NBc], F32)
    zs_ = T(kb, "zs_", [128, NBc], F32)
    gt = T(kb, "gt", [128, NBc], F32)
    t5 = T(kb, "t5", [128, NBc], F32)
    ysb = T(kb, "ysb", [128, 4, NBc], BF16)
    yab2 = T(kb, "yab2", [128, 4, NBc], BF16)
    sgs = Ring([T(kb, f"sgs{i}", [128, NBc], F32) for i in range(2)])
    sga = Ring([T(kb, f"sga{i}", [128, NBc], F32) for i in range(2)])
    m1 = Ring([T(kb, f"m1{i}", [128, NBc], F32) for i in range(2)])
    m2 = Ring([T(kb, f"m2{i}", [128, NBc], F32) for i in range(2)])
    mgT = T(kb, "mgT", [128, 8, NBc], BF16)
    fin = Ring([T(kb, f"fin{i}", [128, 1024], F32) for i in range(1)])
    st2 = T(kb, "st2", [128, 2, 6], F32)
    mv2 = T(kb, "mv2", [128, 2], F32)
    sd2 = T(kb, "sd2", [128, 1], F32)
    fo = Ring([T(kb, f"fo{i}", [128, 1024], F32) for i in range(2)])

    blocksC = []
    for s in range(3):
        NBf = cfg.NB[s]
        NB = min(NBc, NBf)
        for j in range(8):
            for hf in range(NBf // NB):
                blocksC.append((s, j, hf, NB))
    ctxC = {}

    def C_load(i):
        s, j, hf, NB = blocksC[i]
        nC, nCo = cfg.nC[s], cfg.nCo[s]
        r0 = j * nC + hf * NB
        oc0 = j * nCo + hf * NB
        cc0 = hf * NB
        xsrc = X(xr[s].a[r0:r0 + NB, :].rearrange("(t p) d -> p t d", p=128), xr[s].b)
        xin = ln_load(xsrc, NB)
        ys, yaC = ysR.next(), yaR.next()
        for mt in range(4):
            kb.dma(ys[:, mt, 0:NB].s(), yD[s][mt, j, :, cc0:cc0 + NB])
            for h2 in range(2):
                kb.dma(yaC[64 * h2:64 * h2 + 64, mt, 0:NB].s(), yaD[s][2 * mt + h2, :, oc0:oc0 + NB])
        ctxC[i] = dict(xin=xin, ys=ys, yaC=yaC)

    def C_norm(i):
        s, j, hf, NB = blocksC[i]
        nt = NB // 128
        c = ctxC[i]
        res = resR.next()
        c["xc"] = ln_norm(c["xin"], NB, want_f32=res)
        for ti in range(nt):
            kb.tt("pool", res[:, ti, :], res[:, ti, :], agb[:, :], ALU.mult)
            kb.tt("pool", res[:, ti, :], res[:, ti, :], abb[:, :], ALU.add)
        c["res"] = res

    def C_n2(i):
        s, j, hf, NB = blocksC[i]
        c = ctxC[i]
        c["xcT"] = ln_tr(c["xc"], NB)

    def C_main(i):
        s, j, hf, NB = blocksC[i]
        nC, nCo = cfg.nC[s], cfg.nCo[s]
        nt = NB // 128
        oc0 = j * nCo + hf * NB
        c = ctxC.pop(i)
        xcT, res, ys, yaC = c["xcT"], c["res"], c["ys"], c["yaC"]
        if True:
            for mt in range(4):
                ps = pmR.next()
                for k in range(4):
                    kb.mm(ps[:, 0:NB], wglu[:, k, mt * 128:(mt + 1) * 128], ys[:, k, 0:NB], start=(k == 0), stop=(k == 3))
                kb.act(gt[:, 0:NB], ps[:, 0:NB], AF.Sigmoid, bias=bglu[:, mt:mt + 1])
                pz = proj(xcT, WC, mt * 128, 128, NB)
                sl = SLOT[("zs", mt)]
                kb.act(zv[:, 0:NB], pz[:, 0:NB], AF.Identity, bias=hbT[:, sl:sl + 1])
                kb.act(zs_[:, 0:NB], pz[:, 0:NB], AF.Sigmoid, bias=hbT[:, sl:sl + 1])
                kb.tt("dve", t5[:, 0:NB], zv[:, 0:NB], zs_[:, 0:NB], ALU.mult)
                kb.tt("pool", gt[:, 0:NB], gt[:, 0:NB], ys[:, mt, 0:NB], ALU.mult)
                kb.tt("dve", ysb[:, mt, 0:NB].s(), gt[:, 0:NB], t5[:, 0:NB], ALU.mult)
            for mt in range(4):
                pz = proj(xcT, WC, 512 + mt * 128, 128, NB)
                sl = SLOT[("za", mt)]
                kb.act(zv[:, 0:NB], pz[:, 0:NB], AF.Identity, bias=hbT[:, sl:sl + 1])
                kb.act(zs_[:, 0:NB], pz[:, 0:NB], AF.Sigmoid, bias=hbT[:, sl:sl + 1])
                kb.tt("dve", t5[:, 0:NB], zv[:, 0:NB], zs_[:, 0:NB], ALU.mult)
                kb.tt("dve", yab2[:, mt, 0:NB].s(), t5[:, 0:NB], yaC[:, mt, 0:NB], ALU.mult)
            for mo in range(8):
                pgs = proj(xcT, WC, 1024 + mo * 128, 128, NB)
                pga = proj(xcT, WC, 2048 + mo * 128, 128, NB)
                s1, s2 = sgs.next(), sga.next()
                kb.act(s1[:, 0:NB], pgs[:, 0:NB], AF.Sigmoid, bias=hbT[:, SLOT[("gs", mo)]:SLOT[("gs", mo)] + 1])
                kb.act(s2[:, 0:NB], pga[:, 0:NB], AF.Sigmoid, bias=hbT[:, SLOT[("ga", mo)]:SLOT[("ga", mo)] + 1])
                pbs, pba = pmR.next(), pmR.next()
                for k in range(4):
                    kb.mm(pbs[:, 0:NB], wbs[:, k, mo * 128:(mo + 1) * 128], ysb[:, k, 0:NB], start=(k == 0), stop=(k == 3))
                for k in range(4):
                    kb.mm(pba[:, 0:NB], wba[:, k, mo * 128:(mo + 1) * 128], yab2[:, k, 0:NB], start=(k == 0), stop=(k == 3))
                a1, a2 = m1.next(), m2.next()
                kb.tt("dve", a1[:, 0:NB], pbs[:, 0:NB], s1[:, 0:NB], ALU.mult)
                kb.tt("dve", a2[:, 0:NB], pba[:, 0:NB], s2[:, 0:NB], ALU.mult)
                kb.tt("pool", mgT[:, mo, 0:NB].s(), a1[:, 0:NB], a2[:, 0:NB], ALU.add)
            for ti in range(nt):
                f = fin.next()
                for hh in range(2):
                    po = pmR.next()
                    for k in range(8):
                        kb.mm(po[:, :], mgT[:, k, ti * 128:(ti + 1) * 128], wo[:, k, hh * 512:(hh + 1) * 512],
                              start=(k == 0), stop=(k == 7))
                    kb.tt("dve", f[:, hh * 512:(hh + 1) * 512].s(), po[:, :], res[:, ti, hh * 512:(hh + 1) * 512], ALU.add)
                for hh in range(2):
                    kb.bn_stats(st2[:, hh, :].s(), f[:, hh * 512:(hh + 1) * 512])
                kb.bn_aggr(mv2[:, :], st2[:, :, :].re("p a b -> p (a b)"))
                kb.act(sd2[:, :], mv2[:, 1:2], AF.Ln, bias=epsl[:, 0:1])
                kb.act(sd2[:, :], sd2[:, :], AF.Exp, scale=-0.5)
                o = fo.next()
                kb.ts("dve", o[:, :], f[:, :], mv2[:, 0:1], sd2[:, 0:1], ALU.subtract, ALU.mult)
                kb.tt("pool", o[:, :], o[:, :], ogb[:, :], ALU.mult)
                kb.tt("pool", o[:, :], o[:, :], obb[:, :], ALU.add)
                kb.dma(outs[s][oc0 + ti * 128:oc0 + (ti + 1) * 128, :].s(), o[:, :])

    nCb = len(blocksC)
    C_load(0)
    if nCb > 1:
        C_load(1)
    C_norm(0)
    C_n2(0)
    for i in range(nCb):
        if i + 2 < nCb:
            C_load(i + 2)
        if i + 1 < nCb:
            C_norm(i + 1)
        C_main(i)
        if i + 1 < nCb:
            C_n2(i + 1)
    kb.pop()
    kb.close()
    return nc


def _perm(L, jb):
    nC = L // 8
    nCo = nC // 4
    j = np.arange(8)[:, None]
    c = np.arange(nC)[None, :]
    tok = 8 * ((c + jb * nCo) % nC) + j
    return tok.reshape(-1)


def _own_perm(L, jb):
    nC = L // 8
    nCo = nC // 4
    j = np.arange(8)[:, None]
    c = np.arange(nCo)[None, :]
    return (8 * (c + jb * nCo) + j).reshape(-1)


def make_in_maps(inputs, cfg):
    f = np.float32
    g = lambda k: np.asarray(inputs[k], dtype=f)
    w_in = g("w_in")[0]
    com = {}
    com["w_in"] = np.ascontiguousarray(w_in)
    com["lng_col"] = np.ascontiguousarray(g("ln_in_g").reshape(8, 128).T)
    b = g("ln_in_b").reshape(8, 128).T
    com["lnb_col2"] = np.ascontiguousarray(np.stack([b, b], -1))
    com["lng_row"] = g("ln_in_g").reshape(1, -1)
    com["lnb_row"] = g("ln_in_b").reshape(1, -1)
    com["lam"] = np.ascontiguousarray(np.stack([g("ssm_a_re_fwd")[0].T, g("ssm_a_im_fwd")[0].T,
                                                g("ssm_a_re_bwd")[0].T, g("ssm_a_im_bwd")[0].T], 1))
    com["ldt"] = np.ascontiguousarray(np.stack([np.broadcast_to(g("ssm_log_dt_fwd")[0][None, :], (64, 32)),
                                                np.broadcast_to(g("ssm_log_dt_bwd")[0][None, :], (64, 32))], 1))
    lam_ = com["lam"]
    ldt_ = com["ldt"]
    com["lamp"] = np.ascontiguousarray(np.concatenate([lam_[:, :, 0::2], lam_[:, :, 1::2]], 0))
    com["ldtp"] = np.ascontiguousarray(np.concatenate([ldt_[:, :, 0::2], ldt_[:, :, 1::2]], 0))
    com["bT"] = np.ascontiguousarray(np.stack([g("ssm_b_re")[0].transpose(1, 0, 2),
                                               g("ssm_b_im")[0].transpose(1, 0, 2)], 1))
    com["cT"] = np.ascontiguousarray(np.stack([g("ssm_c_re_fwd")[0].transpose(2, 0, 1),
                                               g("ssm_c_im_fwd")[0].transpose(2, 0, 1),
                                               g("ssm_c_re_bwd")[0].transpose(2, 0, 1),
                                               g("ssm_c_im_bwd")[0].transpose(2, 0, 1)], 1))
    com["dtile"] = np.ascontiguousarray(np.tile(g("ssm_d")[0].reshape(32, 16).T, (8, 1)))
    com["kvc"] = np.ascontiguousarray(np.broadcast_to(KV_SETS[None, None, :], (64, 32, 40))).astype(f)
    jj = np.repeat(np.arange(8), 16)
    com["maskF"] = (jj[None, :] >= jj[:, None]).astype(f)
    com["maskB"] = (jj[None, :] <= jj[:, None]).astype(f)
    com["iden"] = np.eye(128, dtype=f)
    com["iotaf"] = np.ascontiguousarray(np.broadcast_to(np.arange(2048, dtype=f)[None, :], (64, 2048)))
    inv = (10000.0 ** (-np.arange(16, dtype=np.float32) * 2.0 / 32)).astype(f)
    com["invf"] = np.ascontiguousarray(np.tile(inv, 8).reshape(128, 1))
    sel = np.zeros((128, 64), f)
    sel[64, :] = 1.0
    com["sel"] = sel
    com["w_glu"] = np.ascontiguousarray(g("w_glu")[0])
    com["bglu_col"] = np.ascontiguousarray(g("b_glu")[0].reshape(4, 128).T)
    com["qg_col"] = np.ascontiguousarray(g("q_norm_g")[0].reshape(2, 128).T)
    com["w_uq"] = np.ascontiguousarray(g("w_uq")[0])
    com["kvg_col"] = np.ascontiguousarray(g("kv_norm_g")[0].reshape(128, 1))
    com["w_ukv"] = np.ascontiguousarray(g("w_ukv")[0])
    com["w_bs"] = np.ascontiguousarray(g("w_branch_ssm")[0])
    com["w_ba"] = np.ascontiguousarray(g("w_branch_attn")[0])
    com["w_o"] = np.ascontiguousarray(g("w_o")[0])
    com["lno_g"] = g("ln_g")[0].reshape(1, -1)
    com["lno_b"] = g("ln_b")[0].reshape(1, -1)
    xp, xs = g("x_prompt"), g("x_sample")
    maps = []
    for core in range(8):
        p, jb = core // 4, core % 4
        seqs = [xp[p], xs[2 * p], xs[2 * p + 1]]
        m = dict(com)
        for s in range(3):
            L = cfg.L[s]
            nC, nCo = cfg.nC[s], cfg.nCo[s]
            pr = _perm(L, jb)
            m[f"xr{s}"] = np.ascontiguousarray(seqs[s][pr])
            m[f"pos{s}"] = pr.astype(f).reshape(1, -1)
            ctrue_first = (np.arange(nC) + jb * nCo) % nC
            ctrue_last = (np.arange(nC) + nCo + jb * nCo) % nC
            m[f"mf{s}"] = (ctrue_last != 0).astype(f).reshape(1, -1)
            m[f"mb{s}"] = (ctrue_first != nC - 1).astype(f).reshape(1, -1)
        maps.append(m)
    return maps


def assemble(results, cfg):
    LA, LB = cfg.L[0], cfg.L[1]
    yp = np.zeros((2, LA, D_MODEL), np.float32)
    ys = np.zeros((4, LB, D_MODEL), np.float32)
    for core in range(8):
        p, jb = core // 4, core % 4
        r = results[core]
        yp[p][_own_perm(LA, jb)] = r["out0"]
        ys[2 * p][_own_perm(LB, jb)] = r["out1"]
        ys[2 * p + 1][_own_perm(LB, jb)] = r["out2"]
    return yp, ys


_NC_CACHE = {}


def kernel(**inputs):
    LA = inputs["x_prompt"].shape[1]
    LB = inputs["x_sample"].shape[1]
    cfg = Cfg(LA, LB)
    key = (LA, LB)
    if key not in _NC_CACHE:
        _NC_CACHE[key] = build(cfg)
    nc = _NC_CACHE[key]
    maps = make_in_maps(inputs, cfg)
    res = run_bass_kernel_spmd(nc, maps, core_ids=list(range(8)))
    return assemble(res.results, cfg)
```

```python
import contextlib
import numpy as np
import concourse.bass as bass
import concourse.mybir as mybir
from concourse.bass_utils import run_bass_kernel_spmd

F32 = mybir.dt.float32
BF16 = mybir.dt.bfloat16
I32 = mybir.dt.int32
AF = mybir.ActivationFunctionType
ALU = mybir.AluOpType

D_MODEL = 1024
IN_W = 4000
LN_EPS = 1e-5
RMS_EPS = 1e-6
ALPHA = 2.0 ** 0.25
SCALE = 96.0 ** -0.5
TWO_PI_HI = 6.28125
TWO_PI_LO = 0.0019353071795864769
GELU_K = 1.5957691216057308
PIECES = [("u", 0, 512), ("zs", 512, 512), ("cq", 1024, 256), ("ckv", 1280, 128), ("kr", 1408, 32),
          ("za", 1440, 512), ("gs", 1952, 1024), ("ga", 2976, 1024), ("krr", 4000, 32)]
PCOL = {n: c for n, c, w in PIECES}
KV_SETS = np.concatenate([np.arange(7, -1, -1), np.arange(0, 8), np.arange(1, 9), np.arange(8, 0, -1),
                          -np.arange(0, 8)]).astype(np.float32)

ENGS = ("sp", "act", "pool", "pe", "dve")


class Buf:
    __slots__ = ("name", "lw", "rd", "war")

    def __init__(self, name):
        self.name = name
        self.lw = {}
        self.rd = {}
        self.war = {}


class X:
    __slots__ = ("ap", "b", "sh")

    def __init__(self, ap, b, sh=False):
        self.ap = ap
        self.b = b
        self.sh = sh

    def __getitem__(self, idx):
        return X(self.ap[idx], self.b, self.sh)

    def bc(self, axis, shape):
        return X(self.ap.unsqueeze(axis).broadcast_to(list(shape)), self.b, self.sh)

    def re(self, s, **kw):
        return X(self.ap.rearrange(s, **kw), self.b, self.sh)

    def s(self):
        return X(self.ap, self.b, True)


class T:
    def __init__(self, kb, name, shape, dt, space="sbuf"):
        self.t = kb.tile(name, shape, dt, space)
        self.b = Buf(name)

    def __getitem__(self, idx):
        return X(self.t[idx], self.b)


class D:
    def __init__(self, nc, name, shape, dt, kind="Internal"):
        self.t = nc.dram_tensor(name, list(shape), dt, kind=kind)
        self.a = self.t.ap()
        self.b = Buf(name)

    def __getitem__(self, idx):
        return X(self.a[idx], self.b)


class KB:
    NDMA = {"sp": 22, "pool": 4, "act": 2}

    def __init__(self, nc):
        self.nc = nc
        self.stack = contextlib.ExitStack()
        self.stages = []
        self.ops = {e: [] for e in ENGS}
        self.cnt = {e: 0 for e in ENGS}
        self.known = {e: {} for e in ENGS}
        self.sems = {}
        self.dcnt = {}
        self.rr = {q: 0 for q in self.NDMA}
        for e in ("act", "pool", "pe", "dve"):
            self.sems["c_" + e] = self.stack.enter_context(nc.semaphore("c_" + e))
        for q, n in self.NDMA.items():
            for i in range(n):
                k = f"d_{q}{i}"
                self.sems[k] = self.stack.enter_context(nc.semaphore(k))
                self.dcnt[k] = 0
        self.nops = 0
        self.uid = 0

    def push(self):
        self.stages.append(contextlib.ExitStack())

    def pop(self):
        self.flush()
        self.stages.pop().close()

    def tile(self, name, shape, dt, space="sbuf"):
        f = self.nc.sbuf_tensor if space == "sbuf" else self.nc.psum_tensor
        self.uid += 1
        st = self.stages[-1] if self.stages else self.stack
        return st.enter_context(f(f"{name}_{self.uid}", list(shape), dt))

    def _deps(self, eng, reads, writes):
        deps = {}

        def add(k, v):
            if deps.get(k, 0) < v:
                deps[k] = v
        for b in reads:
            for k, v in b.lw.items():
                add(k, v)
        for w in writes:
            b, sh = (w.b, w.sh) if isinstance(w, X) else (w, False)
            if sh:
                if b.rd:
                    b.war = dict(b.lw)
                    for k, v in b.rd.items():
                        if b.war.get(k, 0) < v:
                            b.war[k] = v
                    b.lw = {}
                    b.rd = {}
                for k, v in b.war.items():
                    add(k, v)
            else:
                for dct in (b.lw, b.rd, b.war):
                    for k, v in dct.items():
                        add(k, v)
        waits = []
        for k, v in deps.items():
            if k == "c_" + eng and eng == "pe":
                continue
            if self.known[eng].get(k, 0) >= v:
                continue
            self.known[eng][k] = v
            waits.append((k, v))
        return waits

    def _post(self, ev, reads, writes):
        k, v = ev
        for w in writes:
            b, sh = (w.b, w.sh) if isinstance(w, X) else (w, False)
            if sh:
                if b.lw.get(k, 0) < v:
                    b.lw[k] = v
            else:
                b.lw = {k: v}
                b.rd = {}
                b.war = {k: v}
        for b in reads:
            if b.rd.get(k, 0) < v:
                b.rd[k] = v

    def op(self, eng, fn, reads=(), writes=()):
        waits = self._deps(eng, reads, writes)
        self.cnt[eng] += 1
        ev = ("c_" + eng, self.cnt[eng])
        self.ops[eng].append((fn, waits, ev[0], 1))
        self._post(ev, reads, writes)
        self.nops += 1

    def dma(self, out, in_, q="sp", **kw):
        i = self.rr[q]
        self.rr[q] = (i + 1) % self.NDMA[q]
        k = f"d_{q}{i}"
        reads, writes = [in_.b], [out]
        waits = self._deps(q, reads, writes)
        prev = self.dcnt[k]
        if prev > 0 and self.known[q].get(k, 0) < prev:
            self.known[q][k] = prev
            waits.append((k, prev))
        self.dcnt[k] = prev + 16
        ev = (k, prev + 16)
        oa, ia = out.ap, in_.ap
        self.ops[q].append((lambda e: e.dma_start(out=oa, in_=ia, **kw), waits, k, 16))
        self._post(ev, reads, writes)
        self.nops += 1

    def barrier(self):
        tgt = {}
        for e in ("act", "pool", "pe", "dve"):
            if self.cnt[e] > 0:
                tgt["c_" + e] = self.cnt[e]
        for k, v in self.dcnt.items():
            if v > 0:
                tgt[k] = v
        for e in ENGS:
            waits = []
            for k, v in tgt.items():
                if k == "c_" + e:
                    continue
                if self.known[e].get(k, 0) >= v:
                    continue
                self.known[e][k] = v
                waits.append((k, v))
            if waits:
                self.ops[e].append((None, waits, None, 0))

    def flush(self):
        self.barrier()
        nc, ops, sems = self.nc, self.ops, self.sems

        def run(e, lst):
            for fn, waits, sk, inc in lst:
                for k, v in waits:
                    e.wait_ge(sems[k], v)
                if fn is not None:
                    fn(e).then_inc(sems[sk], inc)
        with nc.Block() as block:
            @block.sync
            def _(e):
                run(e, ops["sp"])

            @block.scalar
            def _(e):
                run(e, ops["act"])

            @block.gpsimd
            def _(e):
                run(e, ops["pool"])

            @block.tensor
            def _(e):
                run(e, ops["pe"])

            @block.vector
            def _(e):
                run(e, ops["dve"])
        self.ops = {e: [] for e in ENGS}

    def close(self):
        self.stack.close()

    def mm(self, out, lhsT, rhs, start=True, stop=True):
        o, l, r = out.ap, lhsT.ap, rhs.ap
        self.op("pe", lambda e: e.matmul(o, lhsT=l, rhs=r, start=start, stop=stop),
                reads=[lhsT.b, rhs.b], writes=[out])

    def tr(self, out, in_, ident):
        o, i, d = out.ap, in_.ap, ident.ap
        self.op("pe", lambda e: e.transpose(out=o, in_=i, identity=d), reads=[in_.b, ident.b], writes=[out])

    def act(self, out, in_, func, bias=None, scale=1.0):
        o, i = out.ap, in_.ap
        rd = [in_.b]
        kw = {}
        if bias is not None:
            if isinstance(bias, X):
                kw["bias"] = bias.ap
                rd.append(bias.b)
            else:
                kw["bias"] = bias
        if isinstance(scale, X):
            rd.append(scale.b)
            scale = scale.ap
        self.op("act", lambda e: e.activation(out=o, in_=i, func=func, scale=scale, **kw), reads=rd, writes=[out])

    def tt(self, eng, out, in0, in1, op):
        o, a, b = out.ap, in0.ap, in1.ap
        self.op(eng, lambda e: e.tensor_tensor(out=o, in0=a, in1=b, op=op), reads=[in0.b, in1.b], writes=[out])

    def ts(self, eng, out, in0, s1, s2, op0, op1=None):
        o, a = out.ap, in0.ap
        rd = [in0.b]
        if isinstance(s1, X):
            rd.append(s1.b)
            s1 = s1.ap
        if isinstance(s2, X):
            rd.append(s2.b)
            s2 = s2.ap
        if op1 is None:
            self.op(eng, lambda e: e.tensor_scalar(out=o, in0=a, scalar1=s1, scalar2=None, op0=op0), reads=rd,
                    writes=[out])
        else:
            self.op(eng, lambda e: e.tensor_scalar(out=o, in0=a, scalar1=s1, scalar2=s2, op0=op0, op1=op1),
                    reads=rd, writes=[out])

    def stt(self, eng, out, in0, scalar, in1, op0, op1):
        o, a, b = out.ap, in0.ap, in1.ap
        rd = [in0.b, in1.b]
        if isinstance(scalar, X):
            rd.append(scalar.b)
            scalar = scalar.ap
        self.op(eng, lambda e: e.scalar_tensor_tensor(out=o, in0=a, scalar=scalar, in1=b, op0=op0, op1=op1),
                reads=rd, writes=[out])

    def cp(self, eng, out, in_):
        o, i = out.ap, in_.ap
        if eng == "act":
            self.op("act", lambda e: e.activation(out=o, in_=i, func=AF.Copy), reads=[in_.b], writes=[out])
        else:
            self.op(eng, lambda e: e.tensor_copy(out=o, in_=i), reads=[in_.b], writes=[out])

    def memset(self, eng, out, val):
        o = out.ap
        self.op(eng, lambda e: e.memset(o, val), writes=[out])

    def recip(self, out, in_):
        o, i = out.ap, in_.ap
        self.op("dve", lambda e: e.reciprocal(out=o, in_=i), reads=[in_.b], writes=[out])

    def scan(self, out, d0, d1, initial=0.0):
        o, a, b = out.ap, d0.ap, d1.ap
        self.op("dve", lambda e: e.tensor_tensor_scan(out=o, data0=a, data1=b, initial=initial, op0=ALU.mult,
                                                      op1=ALU.add), reads=[d0.b, d1.b], writes=[out])

    def bn_stats(self, out, in_):
        o, i = out.ap, in_.ap
        self.op("dve", lambda e: e.bn_stats(out=o, in_=i), reads=[in_.b], writes=[out])

    def bn_aggr(self, out, in_):
        o, i = out.ap, in_.ap
        self.op("dve", lambda e: e.bn_aggr(out=o, in_=i), reads=[in_.b], writes=[out])


class Ring:
    def __init__(self, tiles):
        self.tiles = tiles
        self.i = 0

    def next(self):
        t = self.tiles[self.i % len(self.tiles)]
        self.i += 1
        return t


class Cfg:
    def __init__(self, LA=16384, LB=8192, debug=False):
        self.L = [LA, LB, LB]
        self.nC = [l // 8 for l in self.L]
        self.nCo = [c // 4 for c in self.nC]
        self.Lo = [l // 4 for l in self.L]
        self.NB = list(self.nCo)
        for nb in self.NB:
            assert nb % 128 == 0 and nb <= 512
        self.debug = debug


def sincos(kb, eng2, ang, s_out, c_out, tf, ti, tr, th, r_out=None):
    kb.ts("dve", tf, ang, 1.0 / (2 * np.pi), None, ALU.mult)
    kb.cp("dve", ti, tf)
    kb.cp("dve", tf, ti)
    kb.stt("dve", tr, tf, -TWO_PI_HI, ang, ALU.mult, ALU.add)
    kb.stt("dve", tr, tf, -TWO_PI_LO, tr, ALU.mult, ALU.add)
    kb.ts("dve", tr, tr, 3.141592, -3.141592, ALU.min, ALU.max)
    if r_out is not None:
        kb.cp(eng2, r_out, tr)
    if s_out is not None:
        kb.act(s_out, tr, AF.Sin)
        kb.act(th, tr, AF.Sin, scale=0.5)
        if eng2 == "act":
            kb.act(th, th, AF.Square)
            kb.act(c_out, th, AF.Identity, scale=-2.0, bias=1.0)
        else:
            kb.tt(eng2, th, th, th, ALU.mult)
            kb.ts(eng2, c_out, th, -2.0, 1.0, ALU.mult, ALU.add)


def build(cfg):
    nc = bass.Bass("TRN2", target_bir_lowering=False)
    dbg = cfg.debug
    skind = "ExternalOutput" if dbg else "Internal"
    I = {}

    def inp(name, shape):
        I[name] = D(nc, name, shape, F32, kind="ExternalInput")
        return I[name]
    xr = [inp(f"xr{s}", [cfg.L[s], D_MODEL]) for s in range(3)]
    pos = [inp(f"pos{s}", [1, cfg.L[s]]) for s in range(3)]
    mfm = [inp(f"mf{s}", [1, cfg.nC[s]]) for s in range(3)]
    mbm = [inp(f"mb{s}", [1, cfg.nC[s]]) for s in range(3)]
    w_in = inp("w_in", [D_MODEL, IN_W])
    lng_col = inp("lng_col", [128, 8])
    lnb_col2 = inp("lnb_col2", [128, 8, 2])
    lng_row = inp("lng_row", [1, D_MODEL])
    lnb_row = inp("lnb_row", [1, D_MODEL])
    lam = inp("lam", [64, 4, 32])
    ldt = inp("ldt", [64, 2, 32])
    lamp = inp("lamp", [128, 4, 16])
    ldtp = inp("ldtp", [128, 2, 16])
    bT = inp("bT", [64, 2, 32, 16])
    cT = inp("cT", [64, 4, 32, 16])
    dtile = inp("dtile", [128, 32])
    kvc = inp("kvc", [64, 32, 40])
    maskF = inp("maskF", [128, 128])
    maskB = inp("maskB", [128, 128])
    iden = inp("iden", [128, 128])
    iotaf = inp("iotaf", [64, 2048])
    invf = inp("invf", [128, 1])
    sel = inp("sel", [128, 64])
    w_glu = inp("w_glu", [512, 512])
    bglu_col = inp("bglu_col", [128, 4])
    qg_col = inp("qg_col", [128, 2])
    w_uq = inp("w_uq", [256, 768])
    kvg_col = inp("kvg_col", [128, 1])
    w_ukv = inp("w_ukv", [128, 1024])
    w_bs = inp("w_bs", [512, 1024])
    w_ba = inp("w_ba", [512, 1024])
    w_o = inp("w_o", [1024, 1024])
    lno_g = inp("lno_g", [1, D_MODEL])
    lno_b = inp("lno_b", [1, D_MODEL])
    outs = [D(nc, f"out{s}", [cfg.Lo[s], D_MODEL], F32, kind="ExternalOutput") for s in range(3)]

    WD = D(nc, "WD", [128, 8, IN_W + 32], BF16, skind)
    WSD = D(nc, "WSD", [128, 32, 2, 2, 64], BF16, skind)
    TOED = D(nc, "TOED", [128, 32, 128], BF16, skind)
    CLD = D(nc, "CLD", [64, 32, 2, 2, 128], BF16, skind)
    uD = [D(nc, f"uD{s}", [4, 8, 128, cfg.nC[s]], BF16, skind) for s in range(3)]
    kvD = [D(nc, f"kvD{s}", [128, cfg.L[s]], BF16, skind) for s in range(3)]
    krD = [D(nc, f"krD{s}", [32, cfg.L[s]], BF16, skind) for s in range(3)]
    qD = [D(nc, f"qD{s}", [8, 96, cfg.Lo[s]], BF16, skind) for s in range(3)]
    yD = [D(nc, f"yD{s}", [4, 8, 128, cfg.nCo[s]], BF16, skind) for s in range(3)]
    yaD = [D(nc, f"yaD{s}", [8, 64, cfg.Lo[s]], BF16, skind) for s in range(3)]

    kb = KB(nc)
    ident = T(kb, "ident", [128, 128], F32)
    identb = T(kb, "identb", [128, 128], BF16)
    hbT = T(kb, "hbT", [128, 40], F32)
    epsl = T(kb, "epsl", [128, 1], F32)
    epsr = T(kb, "epsr", [128, 1], F32)
    invc = T(kb, "invc", [128, 1], F32)
    rho8 = T(kb, "rho8", [64, 2, 32], F32)
    phi8 = T(kb, "phi8", [64, 2, 32], F32)
    rho8p = T(kb, "rho8p", [128, 2, 16], F32)
    phi8p = T(kb, "phi8p", [128, 2, 16], F32)
    wuq = T(kb, "wuq", [128, 2, 768], BF16)
    wuqr = T(kb, "wuqr", [128, 2, 768], BF16)
    wukv = T(kb, "wukv", [128, 1024], BF16)
    bglu = T(kb, "bglu", [128, 4], F32)
    ones_f = T(kb, "ones_f", [128, 128], F32)

    kb.dma(ident[:, :], iden[:, :])
    kb.dma(identb[:, :], iden[:, :], q="pool")
    kb.dma(invc[:, :], invf[:, :])
    kb.dma(bglu[:, :], bglu_col[:, :])
    kb.memset("dve", epsl[:, :], LN_EPS)
    kb.memset("dve", epsr[:, :], RMS_EPS)
    kb.memset("pool", ones_f[:, :], 1.0)
    kb.memset("pool", hbT[:, :], 0.0)

    SLOTS = []
    for n, c0, w in PIECES:
        for sub in range((w + 127) // 128):
            SLOTS.append((n, sub, c0 + sub * 128, min(128, w - sub * 128)))
    SLOT = {(n, sub): i for i, (n, sub, c, w) in enumerate(SLOTS)}

    kb.push()
    psP = Ring([T(kb, f"psP{i}", [128, 512], F32, "psum") for i in range(4)])
    gcol = T(kb, "gcol", [128, 8], F32)
    bcol2 = T(kb, "bcol2", [128, 8, 2], F32)
    kb.dma(gcol[:, :], lng_col[:, :])
    kb.dma(bcol2[:, :, :], lnb_col2[:, :, :])
    wfR = Ring([T(kb, f"wf{i}", [128, 8, 512], F32) for i in range(2)])
    wbR = Ring([T(kb, f"wb{i}", [128, 8, 512], BF16) for i in range(2)])
    w_in_v = w_in.a.rearrange("(k p) c -> p k c", p=128)
    wf_kr = T(kb, "wf_kr", [128, 8, 32], F32)
    groups = []
    for si, (n, sub, c0, w) in enumerate(SLOTS):
        if groups and groups[-1][0] == n and len(groups[-1][3]) < 4:
            g_ = groups[-1]
            groups[-1] = (n, g_[1], g_[2] + w, g_[3] + [si])
        else:
            groups.append((n, c0, w, [si]))
    for (n, gc0, gw, sis) in groups:
        wf = wfR.next()
        if n != "krr":
            kb.dma(wf[:, :, 0:gw], X(w_in_v[:, :, gc0:gc0 + gw], w_in.b))
            if n == "kr":
                kb.cp("pool", wf_kr[:, :, :], wf[:, :, 0:32])
        else:
            kb.ts("pool", wf[:, :, 0:16], wf_kr[:, :, 16:32], -1.0, None, ALU.mult)
            kb.cp("pool", wf[:, :, 16:32], wf_kr[:, :, 0:16])
        for si in sis:
            _, _, c0, w = SLOTS[si]
            o0 = c0 - gc0
            ps = psP.next()
            for k in range(8):
                kb.mm(ps[0:w, 0:2], wf[:, k, o0:o0 + w], bcol2[:, k, :], start=(k == 0), stop=(k == 7))
            kb.cp("act", hbT[0:w, si:si + 1], ps[0:w, 0:1])
        wb = wbR.next()
        kb.tt("dve", wb[:, :, 0:gw], wf[:, :, 0:gw], gcol[:, :].bc(2, [128, 8, gw]), ALU.mult)
        kb.dma(WD[:, :, gc0:gc0 + gw].s(), wb[:, :, 0:gw])
    qgc = T(kb, "qgc", [128, 2], F32)
    kvgc = T(kb, "kvgc", [128, 1], F32)
    wuqf = T(kb, "wuqf", [128, 2, 768], F32)
    wukvf = T(kb, "wukvf", [128, 1024], F32)
    kb.dma(qgc[:, :], qg_col[:, :])
    kb.dma(kvgc[:, :], kvg_col[:, :])
    kb.dma(wuqf[:, :, :], X(w_uq.a.rearrange("(k p) c -> p k c", p=128), w_uq.b))
    kb.dma(wukvf[:, :], w_ukv[:, :])
    kb.tt("dve", wuqf[:, :, :], wuqf[:, :, :], qgc[:, :].bc(2, [128, 2, 768]), ALU.mult)
    kb.cp("dve", wuq[:, :, :], wuqf[:, :, :])
    kb.memset("pool", wuqr[:, :, :], 0.0)
    wq4 = wuqf[:, :, :].re("p k (h c) -> p k h c", c=96)
    wr4 = wuqr[:, :, :].re("p k (h c) -> p k h c", c=96)
    kb.ts("pool", wr4[:, :, :, 64:80], wq4[:, :, :, 80:96], -1.0, None, ALU.mult)
    kb.cp("pool", wr4[:, :, :, 80:96], wq4[:, :, :, 64:80])
    kb.ts("dve", wukv[:, :], wukvf[:, :], kvgc[:, 0:1], None, ALU.mult)
    kb.pop()
    kb.push()
    psP = Ring([T(kb, f"psP{i}", [128, 512], F32, "psum") for i in range(4)])

    LAM = T(kb, "LAM", [64, 4, 32], F32)
    LDT = T(kb, "LDT", [64, 2, 32], F32)
    BT = T(kb, "BT", [64, 2, 32, 16], F32)
    CTt = T(kb, "CTt", [64, 4, 32, 16], F32)
    KVt = T(kb, "KVt", [64, 32, 40], F32)
    DTL = T(kb, "DTL", [128, 32], F32)
    MKF = T(kb, "MKF", [128, 128], F32)
    MKB = T(kb, "MKB", [128, 128], F32)
    kb.dma(LAM[:, :, :], lam[:, :, :])
    kb.dma(LDT[:, :, :], ldt[:, :, :])
    kb.dma(BT[:, :, :, :], bT[:, :, :, :])
    kb.dma(CTt[:, :, :, :], cT[:, :, :, :])
    kb.dma(KVt[:, :, :], kvc[:, :, :])
    kb.dma(DTL[:, :], dtile[:, :])
    kb.dma(MKF[:, :], maskF[:, :])
    kb.dma(MKB[:, :], maskB[:, :])
    PW = [T(kb, f"PW{d}", [64, 2, 32, 40], F32) for d in range(2)]
    BZ = [T(kb, f"BZ{d}", [64, 2, 32, 16], F32) for d in range(2)]
    kb.push()
    sc = [T(kb, f"sc{i}", [64, 32, 40], F32) for i in range(6)]
    sci = T(kb, "sci", [64, 32, 40], I32)
    sm = [T(kb, f"sm{i}", [64, 32], F32) for i in range(10)]
    smi = T(kb, "smi", [64, 32], I32)
    tmpb = T(kb, "tmpb", [64, 32, 16], F32)
    for d in range(2):
        DTt, RD, TH = sm[0], sm[1], sm[2]
        kb.act(DTt[:, :], LDT[:, d, :], AF.Exp)
        kb.tt("dve", RD[:, :], LAM[:, 2 * d, :], DTt[:, :], ALU.mult)
        kb.tt("dve", TH[:, :], LAM[:, 2 * d + 1, :], DTt[:, :], ALU.mult)
        ANG, MAG, SN, CS = sc[0], sc[1], sc[2], sc[3]
        kb.tt("dve", ANG[:, :, :], KVt[:, :, :], TH[:, :].bc(2, [64, 32, 40]), ALU.mult)
        kb.tt("dve", MAG[:, :, :], KVt[:, :, :], RD[:, :].bc(2, [64, 32, 40]), ALU.mult)
        kb.act(MAG[:, :, :], MAG[:, :, :], AF.Exp)
        sincos(kb, "pool", ANG[:, :, :], SN[:, :, :], CS[:, :, :], sc[4][:, :, :], sci[:, :, :], sc[5][:, :, :],
               sc[0][:, :, :])
        kb.tt("dve", PW[d][:, 0, :, :], MAG[:, :, :], CS[:, :, :], ALU.mult)
        kb.tt("dve", PW[d][:, 1, :, :], MAG[:, :, :], SN[:, :, :], ALU.mult)
        kb.act(rho8[:, d, :], RD[:, :], AF.Exp, scale=8.0)
        A8 = sm[3]
        kb.ts("dve", A8[:, :], TH[:, :], 8.0, None, ALU.mult)
        sincos(kb, "pool", A8[:, :], None, None, sm[4][:, :], smi[:, :], sm[5][:, :], None, r_out=phi8[:, d, :])
        nre, d2, t1, t2, zre, zim = sm[4], sm[5], sm[6], sm[7], sm[8], sm[9]
        are, aim = LAM[:, 2 * d, :], LAM[:, 2 * d + 1, :]
        lbre, lbim = PW[d][:, 0, :, 16], PW[d][:, 1, :, 16]
        kb.ts("dve", nre[:, :], lbre, -1.0, None, ALU.add)
        kb.tt("dve", d2[:, :], are, are, ALU.mult)
        kb.tt("dve", t1[:, :], aim, aim, ALU.mult)
        kb.tt("dve", d2[:, :], d2[:, :], t1[:, :], ALU.add)
        kb.recip(d2[:, :], d2[:, :])
        kb.tt("dve", t1[:, :], nre[:, :], are, ALU.mult)
        kb.tt("dve", t2[:, :], lbim, aim, ALU.mult)
        kb.tt("dve", t1[:, :], t1[:, :], t2[:, :], ALU.add)
        kb.tt("dve", zre[:, :], t1[:, :], d2[:, :], ALU.mult)
        kb.tt("dve", t1[:, :], lbim, are, ALU.mult)
        kb.tt("dve", t2[:, :], nre[:, :], aim, ALU.mult)
        kb.tt("dve", t1[:, :], t1[:, :], t2[:, :], ALU.subtract)
        kb.tt("dve", zim[:, :], t1[:, :], d2[:, :], ALU.mult)
        zreb, zimb = zre[:, :].bc(2, [64, 32, 16]), zim[:, :].bc(2, [64, 32, 16])
        kb.tt("dve", BZ[d][:, 0, :, :], BT[:, 0, :, :], zreb, ALU.mult)
        kb.tt("dve", tmpb[:, :, :], BT[:, 1, :, :], zimb, ALU.mult)
        kb.tt("dve", BZ[d][:, 0, :, :], BZ[d][:, 0, :, :], tmpb[:, :, :], ALU.subtract)
        kb.tt("dve", BZ[d][:, 1, :, :], BT[:, 0, :, :], zimb, ALU.mult)
        kb.tt("dve", tmpb[:, :, :], BT[:, 1, :, :], zreb, ALU.mult)
        kb.tt("dve", BZ[d][:, 1, :, :], BZ[d][:, 1, :, :], tmpb[:, :, :], ALU.add)

    kb.pop()
    LAMp = T(kb, "LAMp", [128, 4, 16], F32)
    LDTp = T(kb, "LDTp", [128, 2, 16], F32)
    kb.dma(LAMp[:, :, :], lamp[:, :, :])
    kb.dma(LDTp[:, :, :], ldtp[:, :, :])
    pp = [T(kb, f"pp{i}", [128, 16], F32) for i in range(6)]
    ppi = T(kb, "ppi", [128, 16], I32)
    for d in range(2):
        kb.act(pp[0][:, :], LDTp[:, d, :], AF.Exp)
        kb.tt("dve", pp[1][:, :], LAMp[:, 2 * d, :], pp[0][:, :], ALU.mult)
        kb.tt("dve", pp[2][:, :], LAMp[:, 2 * d + 1, :], pp[0][:, :], ALU.mult)
        kb.act(rho8p[:, d, :], pp[1][:, :], AF.Exp, scale=8.0)
        kb.ts("dve", pp[3][:, :], pp[2][:, :], 8.0, None, ALU.mult)
        sincos(kb, "pool", pp[3][:, :], None, None, pp[4][:, :], ppi[:, :], pp[5][:, :], None, r_out=phi8p[:, d, :])
    R1 = T(kb, "R1", [64, 2, 32, 8, 16], F32)
    R2 = T(kb, "R2", [64, 2, 32, 8, 16], F32)
    TMP = T(kb, "TMP", [64, 32, 8, 16], F32)
    SH = [64, 32, 8, 16]

    def cout(dst, pw, k0, yre, yim, sign):
        xre = pw[:, 0, :, k0:k0 + 8].bc(3, SH)
        xim = pw[:, 1, :, k0:k0 + 8].bc(3, SH)
        yreb, yimb = yre.bc(2, SH), yim.bc(2, SH)
        kb.tt("dve", dst[:, 0, :, :, :], xre, yreb, ALU.mult)
        kb.tt("pool", TMP[:, :, :, :], xim, yimb, ALU.mult)
        kb.tt("dve", dst[:, 0, :, :, :], dst[:, 0, :, :, :], TMP[:, :, :, :], ALU.subtract)
        kb.tt("dve", dst[:, 1, :, :, :], xre, yimb, ALU.mult)
        kb.tt("pool", TMP[:, :, :, :], xim, yreb, ALU.mult)
        kb.tt("dve", dst[:, 1, :, :, :], dst[:, 1, :, :, :], TMP[:, :, :, :], ALU.add)
        if sign < 0:
            kb.ts("pool", dst[:, 1, :, :, :], dst[:, 1, :, :, :], -1.0, None, ALU.mult)

    wsbR = Ring([T(kb, f"wsb{i}", [128, 4, 2, 64], BF16) for i in range(2)])

    def ws_transposes(src, d):
        for g0 in range(0, 32, 4):
            ps = psP.next()
            for gi in range(4):
                for ri in range(2):
                    kb.tr(ps[:, (gi * 2 + ri) * 64:(gi * 2 + ri + 1) * 64],
                          src[:, ri, g0 + gi, :, :].re("p j q -> p (j q)"), ident[0:64, 0:64])
            wsb = wsbR.next()
            kb.cp("act", wsb[:, :, :, :].re("p g r n -> p (g r n)"), ps[:, :])
            kb.dma(WSD[:, g0:g0 + 4, d, :, :].s(), wsb[:, :, :, :])

    TOE = T(kb, "TOE", [128, 32, 128], F32)
    TOT = T(kb, "TOT", [128, 4, 128], F32)

    def toep(Asrc, Csrc, mask, first):
        for g0 in range(0, 32, 4):
            ps = psP.next()
            for gi in range(4):
                g = g0 + gi
                o = ps[:, gi * 128:(gi + 1) * 128]
                kb.mm(o, Asrc[:, 0, g, :, :].re("p j q -> p (j q)"), Csrc[:, 0, g, :, :].re("p j q -> p (j q)"),
                      start=True, stop=False)
                kb.mm(o, Asrc[:, 1, g, :, :].re("p j q -> p (j q)"), Csrc[:, 1, g, :, :].re("p j q -> p (j q)"),
                      start=False, stop=True)
            mb4 = mask[:, :].bc(1, [128, 4, 128])
            if first:
                kb.tt("dve", TOE[:, g0:g0 + 4, :], ps[:, :].re("p (g c) -> p g c", g=4), mb4, ALU.mult)
            else:
                kb.tt("dve", TOT[:, :, :], ps[:, :].re("p (g c) -> p g c", g=4), mb4, ALU.mult)
                kb.tt("pool", TOE[:, g0:g0 + 4, :], TOE[:, g0:g0 + 4, :], TOT[:, :, :], ALU.add)

    cout(R1, PW[0], 0, BZ[0][:, 0, :, :], BZ[0][:, 1, :, :], +1)
    ws_transposes(R1, 0)
    cout(R1, PW[1], 8, BZ[1][:, 0, :, :], BZ[1][:, 1, :, :], +1)
    ws_transposes(R1, 1)
    cout(R2, PW[1], 32, CTt[:, 2, :, :], CTt[:, 3, :, :], -1)
    toep(R1, R2, MKB, True)
    cout(R1, PW[0], 32, BZ[0][:, 0, :, :], BZ[0][:, 1, :, :], +1)
    cout(R2, PW[0], 8, CTt[:, 0, :, :], CTt[:, 1, :, :], -1)
    toep(R1, R2, MKF, False)
    TOEb = Ring([T(kb, f"TOEb{i}", [128, 4, 128], BF16) for i in range(2)])
    for g0 in range(0, 32, 4):
        kb.tt("dve", TOT[:, :, :], ident[:, :].bc(1, [128, 4, 128]), DTL[:, g0:g0 + 4].bc(2, [128, 4, 128]), ALU.mult)
        kb.tt("dve", TOE[:, g0:g0 + 4, :], TOE[:, g0:g0 + 4, :], TOT[:, :, :], ALU.add)
        tb = TOEb.next()
        kb.cp("act", tb[:, :, :], TOE[:, g0:g0 + 4, :])
        kb.dma(TOED[:, g0:g0 + 4, :].s(), tb[:, :, :])
    CLb = Ring([T(kb, f"CLb{i}", [64, 32, 128], BF16) for i in range(2)])
    cout(R1, PW[0], 16, CTt[:, 0, :, :], CTt[:, 1, :, :], -1)
    cout(R2, PW[1], 24, CTt[:, 2, :, :], CTt[:, 3, :, :], -1)
    for d, R in ((0, R1), (1, R2)):
        for ri in range(2):
            cb_ = CLb.next()
            kb.cp("act" if ri == 0 else "dve", cb_[:, :, :], R[:, ri, :, :, :].re("p g j q -> p g (j q)"))
            kb.dma(CLD[:, :, d, ri, :].s(), cb_[:, :, :])
    kb.pop()

    kb.push()
    NBmax = max(cfg.NB)
    ntmax = NBmax // 128
    WA = T(kb, "WA", [128, 8, 992], BF16)
    kb.dma(WA[:, :, 0:512].s(), WD[:, :, 0:512])
    kb.dma(WA[:, :, 512:768].s(), WD[:, :, 1024:1280])
    kb.dma(WA[:, :, 768:896].s(), WD[:, :, 1280:1408])
    kb.dma(WA[:, :, 896:928].s(), WD[:, :, 1408:1440])
    kb.dma(WA[:, :, 928:960].s(), WD[:, :, 4000:4032])
    xinR = Ring([T(kb, f"xin{i}", [128, ntmax, 1024], F32) for i in range(2)])
    xcR = Ring([T(kb, f"xc{i}", [128, ntmax, 1024], BF16) for i in range(2)])
    xcTR = Ring([T(kb, f"xcT{i}", [128, 8, NBmax], BF16) for i in range(2)])
    stR = Ring([T(kb, f"st{i}", [128, ntmax, 2, 6], F32) for i in range(2)])
    mvR = Ring([T(kb, f"mv{i}", [128, ntmax, 2], F32) for i in range(2)])
    sdR = Ring([T(kb, f"sd{i}", [128, ntmax], F32) for i in range(2)])
    pTR = Ring([T(kb, f"pT{i}", [128, 8, 128], BF16, "psum") for i in range(2)])
    pmR = Ring([T(kb, f"pm{i}", [128, 512], F32, "psum") for i in range(6)])
    ubR = Ring([T(kb, f"ub{i}", [128, 4, NBmax], BF16) for i in range(2)])
    posR = Ring([T(kb, f"posb{i}", [128, NBmax], F32) for i in range(2)])
    rtf = [T(kb, f"rtf{i}", [128, NBmax], F32) for i in range(4)]
    rti = T(kb, "rti", [128, NBmax], I32)
    cosR = Ring([T(kb, f"cosT{i}", [128, NBmax], F32) for i in range(2)])
    sinR = Ring([T(kb, f"sinT{i}", [128, NBmax], F32) for i in range(2)])
    ckvf = T(kb, "ckvf", [128, NBmax], F32)
    sqf = T(kb, "sqf", [128, NBmax], F32)
    rsf = T(kb, "rsf", [128, NBmax], F32)
    ckvbR = Ring([T(kb, f"ckvb{i}", [128, NBmax], BF16) for i in range(2)])
    krt = [T(kb, f"krt{i}", [32, NBmax], F32) for i in range(2)]
    krbR = Ring([T(kb, f"krb{i}", [32, NBmax], BF16) for i in range(2)])
    cqf = T(kb, "cqf", [128, 2, NBmax], F32)
    sq2 = T(kb, "sq2", [128, 2, NBmax], F32)
    cqb = T(kb, "cqb", [128, 2, NBmax], BF16)
    qtA = Ring([T(kb, f"qtA{i}", [96, NBmax], F32) for i in range(2)])
    qtB = Ring([T(kb, f"qtB{i}", [96, NBmax], F32) for i in range(2)])
    rsq = T(kb, "rsq", [128, NBmax], F32)
    qbR = Ring([T(kb, f"qb{i}", [96, NBmax], BF16) for i in range(3)])

    def ln_load(xsrc, NB):
        nt = NB // 128
        xin = xinR.next()
        kb.dma(xin[:, 0:nt, :], xsrc)
        return xin

    def ln_norm(xin, NB, want_f32=None):
        nt = NB // 128
        xc, st, mv, sd = xcR.next(), stR.next(), mvR.next(), sdR.next()
        for ti in range(nt):
            for hh in range(2):
                kb.bn_stats(st[:, ti, hh, :], xin[:, ti, hh * 512:(hh + 1) * 512])
            kb.bn_aggr(mv[:, ti, :], st[:, ti, :, :].re("p a b -> p (a b)"))
        kb.act(sd[:, 0:nt], mv[:, 0:nt, 1], AF.Ln, bias=epsl[:, 0:1])
        kb.act(sd[:, 0:nt], sd[:, 0:nt], AF.Exp, scale=-0.5)
        kb.ts("dve", mv[:, 0:nt, 0], mv[:, 0:nt, 0], -1.0, None, ALU.mult)
        for ti in range(nt):
            eng = "dve" if ti % 2 == 0 else "pool"
            kb.ts(eng, xc[:, ti, :], xin[:, ti, :], mv[:, ti, 0:1], sd[:, ti:ti + 1], ALU.add, ALU.mult)
            if want_f32 is not None:
                kb.ts("pool", want_f32[:, ti, :], xin[:, ti, :], mv[:, ti, 0:1], sd[:, ti:ti + 1], ALU.add,
                      ALU.mult)
        return xc

    def ln_tr(xc, NB):
        nt = NB // 128
        xcT = xcTR.next()
        for ti in range(nt):
            pT = pTR.next()
            for k in range(8):
                kb.tr(pT[:, k, :], xc[:, ti, k * 128:(k + 1) * 128], identb[:, :])
            kb.cp("act" if ti % 2 == 0 else "dve", xcT[:, :, ti * 128:(ti + 1) * 128].s(), pT[:, :, :])
        return xcT

    def proj(xcT, wt, wc0, width, NB):
        ps = pmR.next()
        for k in range(8):
            kb.mm(ps[0:width, 0:NB], wt[:, k, wc0:wc0 + width], xcT[:, k, 0:NB], start=(k == 0), stop=(k == 7))
        return ps

    blocksA = [(s, j, cb) for s in range(3) for j in range(8) for cb in range(cfg.nC[s] // cfg.NB[s])]
    ctxA = {}

    def A_load(i):
        s, j, cb = blocksA[i]
        NB, nC = cfg.NB[s], cfg.nC[s]
        r0 = j * nC + cb * NB
        xsrc = X(xr[s].a[r0:r0 + NB, :].rearrange("(t p) d -> p t d", p=128), xr[s].b)
        xin = ln_load(xsrc, NB)
        posb = posR.next()
        kb.dma(posb[:, 0:NB], X(pos[s].a[0:1, r0:r0 + NB].partition_broadcast(128), pos[s].b))
        ctxA[i] = dict(xin=xin, posb=posb)

    def A_norm(i):
        s, j, cb = blocksA[i]
        NB = cfg.NB[s]
        c = ctxA[i]
        c["xc"] = ln_norm(c["xin"], NB)

    def A_rope(i):
        s, j, cb = blocksA[i]
        NB = cfg.NB[s]
        c = ctxA[i]
        posb, cosT, sinT = c["posb"], cosR.next(), sinR.next()
        kb.ts("dve", rtf[0][:, 0:NB], posb[:, 0:NB], invc[:, 0:1], None, ALU.mult)
        sincos(kb, "pool", rtf[0][:, 0:NB], sinT[:, 0:NB], cosT[:, 0:NB], rtf[1][:, 0:NB], rti[:, 0:NB],
               rtf[2][:, 0:NB], rtf[3][:, 0:NB])
        c["cosT"], c["sinT"] = cosT, sinT

    def A_n2(i):
        s, j, cb = blocksA[i]
        c = ctxA[i]
        c["xcT"] = ln_tr(c["xc"], cfg.NB[s])

    def A_main(i):
        s, j, cb = blocksA[i]
        NB, nC, nCo = cfg.NB[s], cfg.nC[s], cfg.nCo[s]
        r0 = j * nC + cb * NB
        c0 = cb * NB
        own = (cb == 0)
        c = ctxA.pop(i)
        xcT, cosT, sinT = c["xcT"], c["cosT"], c["sinT"]
        ub = ubR.next()
        for mt in range(4):
            ps = proj(xcT, WA, mt * 128, 128, NB)
            kb.act(ub[:, mt, 0:NB], ps[:, 0:NB], AF.Identity, bias=hbT[:, SLOT[("u", mt)]:SLOT[("u", mt)] + 1])
        ps = proj(xcT, WA, 768, 128, NB)
        sl = SLOT[("ckv", 0)]
        kb.act(ckvf[:, 0:NB], ps[:, 0:NB], AF.Identity, bias=hbT[:, sl:sl + 1])
        kb.tt("pool", sqf[:, 0:NB], ckvf[:, 0:NB], ckvf[:, 0:NB], ALU.mult)
        ps = proj(xcT, WA, 896, 32, NB)
        psr = proj(xcT, WA, 928, 32, NB)
        sl, slr = SLOT[("kr", 0)], SLOT[("krr", 0)]
        kb.act(krt[0][:, 0:NB], ps[0:32, 0:NB], AF.Identity, bias=hbT[0:32, sl:sl + 1])
        kb.act(krt[1][:, 0:NB], psr[0:32, 0:NB], AF.Identity, bias=hbT[0:32, slr:slr + 1])
        if own:
            for k2 in range(2):
                ps = proj(xcT, WA, 512 + k2 * 128, 128, NB)
                sl = SLOT[("cq", k2)]
                kb.act(cqf[:, k2, 0:NB], ps[:, 0:NB], AF.Identity, bias=hbT[:, sl:sl + 1])
            kb.tt("pool", sq2[:, :, 0:NB], cqf[:, :, 0:NB], cqf[:, :, 0:NB], ALU.mult)
        for mt in range(4):
            kb.dma(uD[s][mt, j, :, c0:c0 + NB].s(), ub[:, mt, 0:NB])
        if i + 1 < len(blocksA):
            A_rope(i + 1)
        ps2 = pmR.next()
        kb.mm(ps2[:, 0:NB], ones_f[:, :], sqf[:, 0:NB])
        kb.act(rsf[:, 0:NB], ps2[:, 0:NB], AF.Ln, bias=epsr[:, 0:1], scale=1.0 / 128)
        if own:
            ps3 = pmR.next()
            for k2 in range(2):
                kb.mm(ps3[:, 0:NB], ones_f[:, :], sq2[:, k2, 0:NB], start=(k2 == 0), stop=(k2 == 1))
            kb.act(rsq[:, 0:NB], ps3[:, 0:NB], AF.Ln, bias=epsr[:, 0:1], scale=1.0 / 256)
            kb.act(rsq[:, 0:NB], rsq[:, 0:NB], AF.Exp, scale=-0.5)
        kb.act(rsf[:, 0:NB], rsf[:, 0:NB], AF.Exp, scale=-0.5)
        ckvb = ckvbR.next()
        kb.tt("dve", ckvb[:, 0:NB], ckvf[:, 0:NB], rsf[:, 0:NB], ALU.mult)
        kb.dma(kvD[s][:, r0:r0 + NB].s(), ckvb[:, 0:NB])
        kb.tt("dve", krt[0][:, 0:NB], krt[0][:, 0:NB], cosT[0:32, 0:NB], ALU.mult)
        kb.tt("pool", krt[1][:, 0:NB], krt[1][:, 0:NB], sinT[0:32, 0:NB], ALU.mult)
        krb = krbR.next()
        kb.tt("dve", krb[:, 0:NB], krt[0][:, 0:NB], krt[1][:, 0:NB], ALU.add)
        kb.dma(krD[s][:, r0:r0 + NB].s(), krb[:, 0:NB])
        if own:
            oc0 = j * nCo
            kb.tt("dve", cqb[:, :, 0:NB], cqf[:, :, 0:NB], rsq[:, 0:NB].bc(1, [128, 2, NB]), ALU.mult)
            for h in range(8):
                psq, psqr = pmR.next(), pmR.next()
                for k2 in range(2):
                    kb.mm(psq[0:96, 0:NB], wuq[:, k2, h * 96:(h + 1) * 96], cqb[:, k2, 0:NB],
                          start=(k2 == 0), stop=(k2 == 1))
                for k2 in range(2):
                    kb.mm(psqr[0:96, 0:NB], wuqr[:, k2, h * 96:(h + 1) * 96], cqb[:, k2, 0:NB],
                          start=(k2 == 0), stop=(k2 == 1))
                qb = qbR.next()
                qa, qc = qtA.next(), qtB.next()
                kb.cp("act", qb[0:64, 0:NB], psq[0:64, 0:NB])
                kb.tt("dve", qa[64:96, 0:NB], psq[64:96, 0:NB], cosT[64:96, 0:NB], ALU.mult)
                kb.tt("dve", qc[64:96, 0:NB], psqr[64:96, 0:NB], sinT[64:96, 0:NB], ALU.mult)
                kb.tt("pool", qb[64:96, 0:NB], qa[64:96, 0:NB], qc[64:96, 0:NB], ALU.add)
                kb.dma(qD[s][h, :, oc0:oc0 + NB].s(), qb[:, 0:NB])

    nA = len(blocksA)
    A_load(0)
    if nA > 1:
        A_load(1)
    A_norm(0)
    A_rope(0)
    A_n2(0)
    for i in range(nA):
        if i + 2 < nA:
            A_load(i + 2)
        if i + 1 < nA:
            A_norm(i + 1)
        A_main(i)
        if i + 1 < nA:
            A_n2(i + 1)
    kb.pop()
    if cfg.debug == "A":
        kb.close()
        return nc

    kb.push()
    nCm = max(cfg.nC)
    nCom = max(cfg.nCo)
    WSr = Ring([T(kb, f"WSr{i}", [128, 2, 2, 2, 64], BF16) for i in range(2)])
    TOEr = Ring([T(kb, f"TOEr{i}", [128, 2, 128], BF16) for i in range(2)])
    CLr = Ring([T(kb, f"CLr{i}", [128, 2, 2, 2, 128], BF16) for i in range(2)])
    for cl in CLr.tiles:
        kb.memset("pool", cl[:, :, :, :, :], 0.0)
    IOT = T(kb, "IOT", [128, nCm], F32)
    kb.dma(IOT[0:64, :].s(), iotaf[:, 0:nCm])
    kb.dma(IOT[64:128, :].s(), iotaf[:, 0:nCm])
    MF = [T(kb, f"MF{s}", [128, cfg.nC[s]], BF16) for s in range(3)]
    MB = [T(kb, f"MB{s}", [128, cfg.nC[s]], BF16) for s in range(3)]
    for s in range(3):
        kb.dma(MF[s][:, :], X(mfm[s].a[0:1, :].partition_broadcast(128), mfm[s].b), q="pool")
        kb.dma(MB[s][:, :], X(mbm[s].a[0:1, :].partition_broadcast(128), mbm[s].b), q="pool")
    uchR = [[Ring([T(kb, f"uch{s}_{gi}_{i}", [128, cfg.nC[s]], BF16) for i in range(2)])
             for gi in range(2)] for s in range(3)]
    CTr = Ring([T(kb, f"CTb{i}", [128, nCm], F32) for i in range(2)])
    STr = Ring([T(kb, f"STb{i}", [128, nCm], F32) for i in range(2)])
    tabS = {}

    def S_tables(it):
        k_, d_ = divmod(it, 2)
        CTb, STb = CTr.next(), STr.next()
        kb.act(tg[0][:, :], IOT[:, :], AF.Identity, scale=phi8p[:, d_, k_:k_ + 1])
        sincos(kb, "act", tg[0][:, :], STb[:, :], CTb[:, :], tg[1][:, :], tgi[:, :], tg[0][:, :], tg[1][:, :])
        tabS[it] = (CTb, STb)

    tg = [T(kb, f"tg{i}", [128, nCm], F32) for i in range(2)]
    tgi = T(kb, "tgi", [128, nCm], I32)
    fre = T(kb, "Fre", [128, nCm], F32)
    fim = T(kb, "Fim", [128, nCm], F32)
    wre = T(kb, "Wre", [128, nCm], F32)
    wim = T(kb, "Wim", [128, nCm], F32)
    a0 = T(kb, "A0", [128, nCm], F32)
    rt_ = [T(kb, f"rt{i}", [128, 512], F32) for i in range(8)]
    ut_ = [T(kb, f"ut{i}", [128, nCom], F32) for i in range(4)]
    HH = [[Ring([T(kb, f"H{s}_{d}_{i}", [128, 2, cfg.nCo[s]], BF16) for i in range(1)]) for d in range(2)]
          for s in range(3)]
    psS = Ring([T(kb, f"psS{i}", [128, 512], F32, "psum") for i in range(4)])
    psY = Ring([T(kb, f"psY{i}", [128, 512], F32, "psum") for i in range(2)])
    gl = ut_[0:3]
    ygR = Ring([T(kb, f"yg{i}", [128, nCom], BF16) for i in range(2)])
    ctxS = {}

    def S_load(k):
        WSg, TOEg, CLg = WSr.next(), TOEr.next(), CLr.next()
        kb.dma(WSg[:, :, :, :, :], WSD[:, 2 * k:2 * k + 2, :, :, :])
        kb.dma(TOEg[:, :, :], TOED[:, 2 * k:2 * k + 2, :])
        for gi in range(2):
            kb.dma(CLg[64 * gi:64 * gi + 64, gi, :, :, :].s(), CLD[:, 2 * k + gi, :, :, :])
        uch = []
        for s in range(3):
            nC, nCo = cfg.nC[s], cfg.nCo[s]
            us = []
            for gi in range(2):
                g = 2 * k + gi
                u = uchR[s][gi].next()
                for jj in range(8):
                    kb.dma(u[16 * jj:16 * jj + 16, 0:nC].s(), uD[s][g // 8, jj, 16 * (g % 8):16 * (g % 8) + 16, :])
                us.append(u)
            uch.append(us)
        ctxS[k] = (WSg, TOEg, CLg, uch)

    def S_compute(k):
        WSg, TOEg, CLg, uch = ctxS.pop(k)
        Hcur = [[None, None] for _ in range(3)]
        for d in range(2):
            CTb, STb = tabS.pop(2 * k + d)
            for s in range(3):
                nC, nCo = cfg.nC[s], cfg.nCo[s]
                base = nCo if d == 0 else 0
                pend = None
                for pk_i, p0 in enumerate(range(0, nC, 512)):
                    w = min(512, nC - p0)
                    pr, pi = psS.next(), psS.next()
                    src0 = (p0 + base) % nC
                    segs = [(src0, 0, w)] if src0 + w <= nC else [(src0, 0, nC - src0), (0, nC - src0, w - (nC - src0))]
                    for gi in range(2):
                        u = uch[s][gi]
                        for (sc0, oc_, sw) in segs:
                            kb.mm(pr[64 * gi:64 * gi + 64, oc_:oc_ + sw], WSg[:, gi, d, 0, :], u[:, sc0:sc0 + sw])
                            kb.mm(pi[64 * gi:64 * gi + 64, oc_:oc_ + sw], WSg[:, gi, d, 1, :], u[:, sc0:sc0 + sw])
                    ct, st_ = CTb[:, p0:p0 + w], STb[:, p0:p0 + w]
                    r = rt_[(pk_i % 2) * 4:(pk_i % 2) * 4 + 4]
                    kb.tt("dve", r[0][:, 0:w], pr[:, 0:w], ct, ALU.mult)
                    kb.tt("dve", r[1][:, 0:w], pi[:, 0:w], st_, ALU.mult)
                    kb.tt("dve", r[2][:, 0:w], pi[:, 0:w], ct, ALU.mult)
                    kb.tt("dve", r[3][:, 0:w], pr[:, 0:w], st_, ALU.mult)
                    if pend is not None:
                        pr_, pp0, pw = pend
                        kb.tt("dve", fre[:, pp0:pp0 + pw], pr_[0][:, 0:pw], pr_[1][:, 0:pw], ALU.add if d == 0 else ALU.subtract)
                        kb.tt("dve", fim[:, pp0:pp0 + pw], pr_[2][:, 0:pw], pr_[3][:, 0:pw], ALU.subtract if d == 0 else ALU.add)
                    pend = (r, p0, w)
                pr_, pp0, pw = pend
                kb.tt("dve", fre[:, pp0:pp0 + pw], pr_[0][:, 0:pw], pr_[1][:, 0:pw], ALU.add if d == 0 else ALU.subtract)
                kb.tt("dve", fim[:, pp0:pp0 + pw], pr_[2][:, 0:pw], pr_[3][:, 0:pw], ALU.subtract if d == 0 else ALU.add)
                msk = MF[s] if d == 0 else MB[s]
                kb.act(a0[:, 0:nC], msk[:, :], AF.Identity, scale=rho8p[:, d, k:k + 1])
                if d == 0:
                    kb.scan(wre[:, 0:nC], a0[:, 0:nC], fre[:, 0:nC])
                    kb.scan(wim[:, 0:nC], a0[:, 0:nC], fim[:, 0:nC])
                    q0, m0 = nC - nCo - 1, nC - nCo
                else:
                    kb.scan(wre[:, 0:nC][:, ::-1], a0[:, 0:nC][:, ::-1], fre[:, 0:nC][:, ::-1])
                    kb.scan(wim[:, 0:nC][:, ::-1], a0[:, 0:nC][:, ::-1], fim[:, 0:nC][:, ::-1])
                    q0, m0 = 1, 0
                H = HH[s][d].next()
                ct, st_ = CTb[:, q0:q0 + nCo], STb[:, q0:q0 + nCo]
                wr, wi = wre[:, q0:q0 + nCo], wim[:, q0:q0 + nCo]
                mk = msk[:, m0:m0 + nCo]
                kb.tt("dve", ut_[0][:, 0:nCo], ct, wr, ALU.mult)
                kb.tt("dve", ut_[1][:, 0:nCo], st_, wi, ALU.mult)
                kb.tt("pool", ut_[2][:, 0:nCo], ct, wi, ALU.mult)
                kb.tt("pool", ut_[3][:, 0:nCo], st_, wr, ALU.mult)
                kb.tt("dve", ut_[0][:, 0:nCo], ut_[0][:, 0:nCo], ut_[1][:, 0:nCo], ALU.subtract if d == 0 else ALU.add)
                kb.tt("pool", ut_[2][:, 0:nCo], ut_[2][:, 0:nCo], ut_[3][:, 0:nCo], ALU.add if d == 0 else ALU.subtract)
                kb.tt("dve", H[:, 0, :], ut_[0][:, 0:nCo], mk, ALU.mult)
                kb.tt("pool", H[:, 1, :], ut_[2][:, 0:nCo], mk, ALU.mult)
                Hcur[s][d] = H
                if s == 0 and 2 * k + d + 1 < 32:
                    S_tables(2 * k + d + 1)
        for s in range(3):
            nC, nCo = cfg.nC[s], cfg.nCo[s]
            for gi in range(2):
                g = 2 * k + gi
                u = uch[s][gi]
                lo, hi = 64 * gi, 64 * gi + 64
                py = psY.next()
                kb.mm(py[:, 0:nCo], TOEg[:, gi, :], u[:, 0:nCo], start=True, stop=False)
                kb.mm(py[:, 0:nCo], CLg[:, gi, 0, 0, :], Hcur[s][0][:, 0, :], start=False, stop=False)
                kb.mm(py[:, 0:nCo], CLg[:, gi, 0, 1, :], Hcur[s][0][:, 1, :], start=False, stop=False)
                kb.mm(py[:, 0:nCo], CLg[:, gi, 1, 0, :], Hcur[s][1][:, 0, :], start=False, stop=False)
                kb.mm(py[:, 0:nCo], CLg[:, gi, 1, 1, :], Hcur[s][1][:, 1, :], start=False, stop=True)
                y = py[:, 0:nCo]
                kb.act(gl[0][:, 0:nCo], y, AF.Square)
                kb.ts("pool", gl[0][:, 0:nCo], gl[0][:, 0:nCo], 0.044715, 1.0, ALU.mult, ALU.add)
                kb.tt("dve", gl[1][:, 0:nCo], gl[0][:, 0:nCo], y, ALU.mult)
                kb.act(gl[2][:, 0:nCo], gl[1][:, 0:nCo], AF.Sigmoid, scale=GELU_K)
                yg = ygR.next()
                kb.tt("dve", yg[:, 0:nCo], gl[2][:, 0:nCo], y, ALU.mult)
                for tt_ in range(8):
                    kb.dma(yD[s][g // 8, tt_, 16 * (g % 8):16 * (g % 8) + 16, :].s(), yg[16 * tt_:16 * tt_ + 16, 0:nCo])

    S_load(0)
    S_tables(0)
    for k in range(16):
        if k + 1 < 16:
            S_load(k + 1)
        S_compute(k)
    kb.pop()
    if cfg.debug == "S":
        kb.close()
        return nc

    kb.push()
    Lm, Lom = max(cfg.L), max(cfg.Lo)
    QBm = min(512, Lom)
    ckvn = T(kb, "ckvn", [128, Lm], BF16)
    KTs = [T(kb, f"KT{i}", [96, Lm], BF16) for i in range(2)]
    Vts = [T(kb, f"Vt{i}", [128, Lm // 128, 65], BF16) for i in range(2)]
    QTs = [T(kb, f"QT{i}", [96, Lom], BF16) for i in range(2)]
    for v in Vts:
        kb.memset("pool", v[:, :, 64:65], 1.0)
    selt = T(kb, "selt", [128, 64], F32)
    kb.dma(selt[:, :], sel[:, :])
    rc = T(kb, "rc", [128, QBm], F32)
    kb.memset("dve", rc[:, :], 0.0)
    psSt = Ring([T(kb, f"psSt{i}", [128, 1024], F32, "psum") for i in range(3)])
    psO = Ring([T(kb, f"psO{i}", [128, 512], F32, "psum") for i in range(2)])
    PT = Ring([T(kb, f"PT{i}", [128, 2 * QBm], BF16) for i in range(3)])
    bcs = T(kb, "bcs", [64, QBm], F32)
    yab = Ring([T(kb, f"yab{i}", [64, QBm], BF16) for i in range(2)])
    DEPTH = 2
    for s in range(3):
        L, Lo = cfg.L[s], cfg.Lo[s]
        QB = min(512, Lo)
        nkt = L // 128
        nkp = nkt // 2
        kb.dma(ckvn[:, 0:L], kvD[s][:, :])
        for KT in KTs:
            kb.dma(KT[64:96, 0:L], krD[s][:, :])

        def build_kv(h):
            KT, V = KTs[h % 2], Vts[h % 2]
            for p0 in range(0, L, 1024):
                pk = psSt.next()
                for e in range(2):
                    kb.mm(pk[0:64, e * 512:(e + 1) * 512], wukv[:, h * 128:h * 128 + 64],
                          ckvn[:, p0 + e * 512:p0 + (e + 1) * 512])
                kb.cp("dve" if (p0 // 1024) % 2 == 0 else "act", KT[0:64, p0:p0 + 1024].s(), pk[0:64, :])
            for kt0 in range(0, nkt, 16):
                pk = psSt.next()
                for i in range(16):
                    kb.mm(pk[:, i * 64:(i + 1) * 64], ckvn[:, (kt0 + i) * 128:(kt0 + i + 1) * 128],
                          wukv[:, h * 128 + 64:h * 128 + 128])
                kb.cp("act" if (kt0 // 16) % 2 == 0 else "dve", V[:, kt0:kt0 + 16, 0:64].s(),
                      pk[:, :].re("p (i c) -> p i c", c=64))
            kb.dma(QTs[h % 2][:, 0:Lo], qD[s][h, :, :])

        build_kv(0)
        for h in range(8):
            if h + 1 < 8:
                build_kv(h + 1)
            KT, V, Q = KTs[h % 2], Vts[h % 2], QTs[h % 2]
            units = [(qb0, kp) for qb0 in range(0, Lo, QB) for kp in range(nkp)]
            pst_of = {}
            po_of = {}
            pending = []

            def qk(i):
                qb0, kp = units[i]
                pst = psSt.next()
                for e in range(2):
                    kt = 2 * kp + e
                    kb.mm(pst[:, e * 512:e * 512 + QB], KT[0:96, kt * 128:(kt + 1) * 128], Q[0:96, qb0:qb0 + QB])
                pst_of[i] = pst

            def fin_pe(qb0, po):
                kb.recip(rc[64:65, 0:QB], po[64:65, 0:QB])
                psB = psSt.next()
                kb.mm(psB[0:64, 0:QB], selt[:, :], rc[:, 0:QB])
                kb.cp("dve", bcs[:, 0:QB], psB[0:64, 0:QB])
                ya = yab.next()
                kb.tt("dve", ya[:, 0:QB], po[0:64, 0:QB], bcs[:, 0:QB], ALU.mult)
                kb.dma(yaD[s][h, :, qb0:qb0 + QB].s(), ya[:, 0:QB])

            for i in range(min(DEPTH, len(units))):
                qk(i)
            for i, (qb0, kp) in enumerate(units):
                if kp == 0:
                    po_of[qb0] = psO.next()
                po = po_of[qb0]
                pst = pst_of.pop(i)
                pt = PT.next()
                if QB == 512:
                    kb.act(pt[:, 0:1024], pst[:, 0:1024], AF.Exp, scale=SCALE)
                else:
                    for e in range(2):
                        kb.act(pt[:, e * 512:e * 512 + QB], pst[:, e * 512:e * 512 + QB], AF.Exp, scale=SCALE)
                if i + DEPTH < len(units):
                    qk(i + DEPTH)
                for e in range(2):
                    kt = 2 * kp + e
                    kb.mm(po[0:65, 0:QB], V[:, kt, 0:65], pt[:, e * 512:e * 512 + QB], start=(kt == 0),
                          stop=(kt == nkt - 1))
                if pending and pending[0][0] <= i:
                    _, fq, fpo = pending.pop(0)
                    fin_pe(fq, fpo)
                if kp == nkp - 1:
                    pending.append((i + 2, qb0, po))
            for _, fq, fpo in pending:
                fin_pe(fq, fpo)
    kb.pop()
    if cfg.debug == "B":
        kb.close()
        return nc

    kb.push()
    WC = T(kb, "WC", [128, 8, 3072], BF16)
    kb.dma(WC[:, :, 0:512].s(), WD[:, :, 512:1024])
    kb.dma(WC[:, :, 512:1024].s(), WD[:, :, 1440:1952])
    kb.dma(WC[:, :, 1024:2048].s(), WD[:, :, 1952:2976])
    kb.dma(WC[:, :, 2048:3072].s(), WD[:, :, 2976:4000])
    wglu = T(kb, "wglu", [128, 4, 512], BF16)
    wbs = T(kb, "wbs", [128, 4, 1024], BF16)
    wba = T(kb, "wba", [128, 4, 1024], BF16)
    wo = T(kb, "wo", [128, 8, 1024], BF16)
    kb.dma(wglu[:, :, :], X(w_glu.a.rearrange("(k p) c -> p k c", p=128), w_glu.b), q="pool")
    kb.dma(wbs[:, :, :], X(w_bs.a.rearrange("(k p) c -> p k c", p=128), w_bs.b), q="pool")
    kb.dma(wba[:, :, :], X(w_ba.a.rearrange("(k p) c -> p k c", p=128), w_ba.b), q="pool")
    kb.dma(wo[:, :, :], X(w_o.a.rearrange("(k p) c -> p k c", p=128), w_o.b), q="pool")
    agb = T(kb, "agb", [128, 1024], F32)
    abb = T(kb, "abb", [128, 1024], F32)
    ogb = T(kb, "ogb", [128, 1024], F32)
    obb = T(kb, "obb", [128, 1024], F32)
    kb.dma(agb[:, :], X(lng_row.a[0:1, :].partition_broadcast(128), lng_row.b))
    kb.dma(abb[:, :], X(lnb_row.a[0:1, :].partition_broadcast(128), lnb_row.b))
    kb.dma(ogb[:, :], X(lno_g.a[0:1, :].partition_broadcast(128), lno_g.b))
    kb.dma(obb[:, :], X(lno_b.a[0:1, :].partition_broadcast(128), lno_b.b))
    kb.ts("dve", agb[:, :], agb[:, :], ALPHA, None, ALU.mult)
    kb.ts("dve", abb[:, :], abb[:, :], ALPHA, None, ALU.mult)
    NBc = min(256, NBmax)
    ntc = NBc // 128
    xinR = Ring([T(kb, f"xinC{i}", [128, ntc, 1024], F32) for i in range(2)])
    xcR = Ring([T(kb, f"xcC{i}", [128, ntc, 1024], BF16) for i in range(2)])
    xcTR = Ring([T(kb, f"xcTC{i}", [128, 8, NBc], BF16) for i in range(2)])
    stR = Ring([T(kb, f"stC{i}", [128, ntc, 2, 6], F32) for i in range(2)])
    mvR = Ring([T(kb, f"mvC{i}", [128, ntc, 2], F32) for i in range(2)])
    sdR = Ring([T(kb, f"sdC{i}", [128, ntc], F32) for i in range(2)])
    pTR = Ring([T(kb, f"pTC{i}", [128, 8, 128], BF16, "psum") for i in range(2)])
    pmR = Ring([T(kb, f"pmC{i}", [128, 512], F32, "psum") for i in range(6)])
    resR = Ring([T(kb, f"res{i}", [128, ntc, 1024], F32) for i in range(2)])
    ysR = Ring([T(kb, f"ys{i}", [128, 4, NBc], BF16) for i in range(3)])
    yaR = Ring([T(kb, f"yaC{i}", [128, 4, NBc], BF16) for i in range(3)])
    zv = T(kb, "zv", [128, NBc], F32)
    zs_ = T(kb, "zs_", [128, NBc], F32)
    gt = T(kb, "gt", [128, NBc], F32)
    t5 = T(kb, "t5", [128, NBc], F32)
    ysb = T(kb, "ysb", [128, 4, NBc], BF16)
    yab2 = T(kb, "yab2", [128, 4, NBc], BF16)
    sgs = Ring([T(kb, f"sgs{i}", [128, NBc], F32) for i in range(2)])
    sga = Ring([T(kb, f"sga{i}", [128, NBc], F32) for i in range(2)])
    m1 = Ring([T(kb, f"m1{i}", [128, NBc], F32) for i in range(2)])
    m2 = Ring([T(kb, f"m2{i}", [128, NBc], F32) for i in range(2)])
    mgT = T(kb, "mgT", [128, 8, NBc], BF16)
    fin = Ring([T(kb, f"fin{i}", [128, 1024], F32) for i in range(1)])
    st2 = T(kb, "st2", [128, 2, 6], F32)
    mv2 = T(kb, "mv2", [128, 2], F32)
    sd2 = T(kb, "sd2", [128, 1], F32)
    fo = Ring([T(kb, f"fo{i}", [128, 1024], F32) for i in range(2)])

    blocksC = []
    for s in range(3):
        NBf = cfg.NB[s]
        NB = min(NBc, NBf)
        for j in range(8):
            for hf in range(NBf // NB):
                blocksC.append((s, j, hf, NB))
    ctxC = {}

    def C_load(i):
        s, j, hf, NB = blocksC[i]
        nC, nCo = cfg.nC[s], cfg.nCo[s]
        r0 = j * nC + hf * NB
        oc0 = j * nCo + hf * NB
        cc0 = hf * NB
        xsrc = X(xr[s].a[r0:r0 + NB, :].rearrange("(t p) d -> p t d", p=128), xr[s].b)
        xin = ln_load(xsrc, NB)
        ys, yaC = ysR.next(), yaR.next()
        for mt in range(4):
            kb.dma(ys[:, mt, 0:NB].s(), yD[s][mt, j, :, cc0:cc0 + NB])
            for h2 in range(2):
                kb.dma(yaC[64 * h2:64 * h2 + 64, mt, 0:NB].s(), yaD[s][2 * mt + h2, :, oc0:oc0 + NB])
        ctxC[i] = dict(xin=xin, ys=ys, yaC=yaC)

    def C_norm(i):
        s, j, hf, NB = blocksC[i]
        nt = NB // 128
        c = ctxC[i]
        res = resR.next()
        c["xc"] = ln_norm(c["xin"], NB, want_f32=res)
        for ti in range(nt):
            kb.tt("pool", res[:, ti, :], res[:, ti, :], agb[:, :], ALU.mult)
            kb.tt("pool", res[:, ti, :], res[:, ti, :], abb[:, :], ALU.add)
        c["res"] = res

    def C_n2(i):
        s, j, hf, NB = blocksC[i]
        c = ctxC[i]
        c["xcT"] = ln_tr(c["xc"], NB)

    def C_main(i):
        s, j, hf, NB = blocksC[i]
        nC, nCo = cfg.nC[s], cfg.nCo[s]
        nt = NB // 128
        oc0 = j * nCo + hf * NB
        c = ctxC.pop(i)
        xcT, res, ys, yaC = c["xcT"], c["res"], c["ys"], c["yaC"]
        if True:
            for mt in range(4):
                ps = pmR.next()
                for k in range(4):
                    kb.mm(ps[:, 0:NB], wglu[:, k, mt * 128:(mt + 1) * 128], ys[:, k, 0:NB], start=(k == 0), stop=(k == 3))
                kb.act(gt[:, 0:NB], ps[:, 0:NB], AF.Sigmoid, bias=bglu[:, mt:mt + 1])
                pz = proj(xcT, WC, mt * 128, 128, NB)
                sl = SLOT[("zs", mt)]
                kb.act(zv[:, 0:NB], pz[:, 0:NB], AF.Identity, bias=hbT[:, sl:sl + 1])
                kb.act(zs_[:, 0:NB], pz[:, 0:NB], AF.Sigmoid, bias=hbT[:, sl:sl + 1])
                kb.tt("dve", t5[:, 0:NB], zv[:, 0:NB], zs_[:, 0:NB], ALU.mult)
                kb.tt("pool", gt[:, 0:NB], gt[:, 0:NB], ys[:, mt, 0:NB], ALU.mult)
                kb.tt("dve", ysb[:, mt, 0:NB], gt[:, 0:NB], t5[:, 0:NB], ALU.mult)
            for mt in range(4):
                pz = proj(xcT, WC, 512 + mt * 128, 128, NB)
                sl = SLOT[("za", mt)]
                kb.act(zv[:, 0:NB], pz[:, 0:NB], AF.Identity, bias=hbT[:, sl:sl + 1])
                kb.act(zs_[:, 0:NB], pz[:, 0:NB], AF.Sigmoid, bias=hbT[:, sl:sl + 1])
                kb.tt("dve", t5[:, 0:NB], zv[:, 0:NB], zs_[:, 0:NB], ALU.mult)
                kb.tt("dve", yab2[:, mt, 0:NB], t5[:, 0:NB], yaC[:, mt, 0:NB], ALU.mult)
            for mo in range(8):
                pgs = proj(xcT, WC, 1024 + mo * 128, 128, NB)
                pga = proj(xcT, WC, 2048 + mo * 128, 128, NB)
                s1, s2 = sgs.next(), sga.next()
                kb.act(s1[:, 0:NB], pgs[:, 0:NB], AF.Sigmoid, bias=hbT[:, SLOT[("gs", mo)]:SLOT[("gs", mo)] + 1])
                kb.act(s2[:, 0:NB], pga[:, 0:NB], AF.Sigmoid, bias=hbT[:, SLOT[("ga", mo)]:SLOT[("ga", mo)] + 1])
                pbs, pba = pmR.next(), pmR.next()
                for k in range(4):
                    kb.mm(pbs[:, 0:NB], wbs[:, k, mo * 128:(mo + 1) * 128], ysb[:, k, 0:NB], start=(k == 0), stop=(k == 3))
                for k in range(4):
                    kb.mm(pba[:, 0:NB], wba[:, k, mo * 128:(mo + 1) * 128], yab2[:, k, 0:NB], start=(k == 0), stop=(k == 3))
                a1, a2 = m1.next(), m2.next()
                kb.tt("dve", a1[:, 0:NB], pbs[:, 0:NB], s1[:, 0:NB], ALU.mult)
                kb.tt("dve", a2[:, 0:NB], pba[:, 0:NB], s2[:, 0:NB], ALU.mult)
                kb.tt("pool", mgT[:, mo, 0:NB], a1[:, 0:NB], a2[:, 0:NB], ALU.add)
            for ti in range(nt):
                f = fin.next()
                for hh in range(2):
                    po = pmR.next()
                    for k in range(8):
                        kb.mm(po[:, :], mgT[:, k, ti * 128:(ti + 1) * 128], wo[:, k, hh * 512:(hh + 1) * 512],
                              start=(k == 0), stop=(k == 7))
                    kb.tt("dve", f[:, hh * 512:(hh + 1) * 512], po[:, :], res[:, ti, hh * 512:(hh + 1) * 512], ALU.add)
                for hh in range(2):
                    kb.bn_stats(st2[:, hh, :], f[:, hh * 512:(hh + 1) * 512])
                kb.bn_aggr(mv2[:, :], st2[:, :, :].re("p a b -> p (a b)"))
                kb.act(sd2[:, :], mv2[:, 1:2], AF.Ln, bias=epsl[:, 0:1])
                kb.act(sd2[:, :], sd2[:, :], AF.Exp, scale=-0.5)
                o = fo.next()
                kb.ts("dve", o[:, :], f[:, :], mv2[:, 0:1], sd2[:, 0:1], ALU.subtract, ALU.mult)
                kb.tt("pool", o[:, :], o[:, :], ogb[:, :], ALU.mult)
                kb.tt("pool", o[:, :], o[:, :], obb[:, :], ALU.add)
                kb.dma(outs[s][oc0 + ti * 128:oc0 + (ti + 1) * 128, :].s(), o[:, :])

    nCb = len(blocksC)
    C_load(0)
    if nCb > 1:
        C_load(1)
    C_norm(0)
    C_n2(0)
    for i in range(nCb):
        if i + 2 < nCb:
            C_load(i + 2)
        if i + 1 < nCb:
            C_norm(i + 1)
        C_main(i)
        if i + 1 < nCb:
            C_n2(i + 1)
    kb.pop()
    kb.close()
    return nc


def _perm(L, jb):
    nC = L // 8
    nCo = nC // 4
    j = np.arange(8)[:, None]
    c = np.arange(nC)[None, :]
    tok = 8 * ((c + jb * nCo) % nC) + j
    return tok.reshape(-1)


def _own_perm(L, jb):
    nC = L // 8
    nCo = nC // 4
    j = np.arange(8)[:, None]
    c = np.arange(nCo)[None, :]
    return (8 * (c + jb * nCo) + j).reshape(-1)


def make_in_maps(inputs, cfg):
    f = np.float32
    g = lambda k: np.asarray(inputs[k], dtype=f)
    w_in = g("w_in")[0]
    com = {}
    com["w_in"] = np.ascontiguousarray(w_in)
    com["lng_col"] = np.ascontiguousarray(g("ln_in_g").reshape(8, 128).T)
    b = g("ln_in_b").reshape(8, 128).T
    com["lnb_col2"] = np.ascontiguousarray(np.stack([b, b], -1))
    com["lng_row"] = g("ln_in_g").reshape(1, -1)
    com["lnb_row"] = g("ln_in_b").reshape(1, -1)
    com["lam"] = np.ascontiguousarray(np.stack([g("ssm_a_re_fwd")[0].T, g("ssm_a_im_fwd")[0].T,
                                                g("ssm_a_re_bwd")[0].T, g("ssm_a_im_bwd")[0].T], 1))
    com["ldt"] = np.ascontiguousarray(np.stack([np.broadcast_to(g("ssm_log_dt_fwd")[0][None, :], (64, 32)),
                                                np.broadcast_to(g("ssm_log_dt_bwd")[0][None, :], (64, 32))], 1))
    lam_ = com["lam"]
    ldt_ = com["ldt"]
    com["lamp"] = np.ascontiguousarray(np.concatenate([lam_[:, :, 0::2], lam_[:, :, 1::2]], 0))
    com["ldtp"] = np.ascontiguousarray(np.concatenate([ldt_[:, :, 0::2], ldt_[:, :, 1::2]], 0))
    com["bT"] = np.ascontiguousarray(np.stack([g("ssm_b_re")[0].transpose(1, 0, 2),
                                               g("ssm_b_im")[0].transpose(1, 0, 2)], 1))
    com["cT"] = np.ascontiguousarray(np.stack([g("ssm_c_re_fwd")[0].transpose(2, 0, 1),
                                               g("ssm_c_im_fwd")[0].transpose(2, 0, 1),
                                               g("ssm_c_re_bwd")[0].transpose(2, 0, 1),
                                               g("ssm_c_im_bwd")[0].transpose(2, 0, 1)], 1))
    com["dtile"] = np.ascontiguousarray(np.tile(g("ssm_d")[0].reshape(32, 16).T, (8, 1)))
    com["kvc"] = np.ascontiguousarray(np.broadcast_to(KV_SETS[None, None, :], (64, 32, 40))).astype(f)
    jj = np.repeat(np.arange(8), 16)
    com["maskF"] = (jj[None, :] >= jj[:, None]).astype(f)
    com["maskB"] = (jj[None, :] <= jj[:, None]).astype(f)
    com["iden"] = np.eye(128, dtype=f)
    com["iotaf"] = np.ascontiguousarray(np.broadcast_to(np.arange(2048, dtype=f)[None, :], (64, 2048)))
    inv = (10000.0 ** (-np.arange(16, dtype=np.float32) * 2.0 / 32)).astype(f)
    com["invf"] = np.ascontiguousarray(np.tile(inv, 8).reshape(128, 1))
    sel = np.zeros((128, 64), f)
    sel[64, :] = 1.0
    com["sel"] = sel
    com["w_glu"] = np.ascontiguousarray(g("w_glu")[0])
    com["bglu_col"] = np.ascontiguousarray(g("b_glu")[0].reshape(4, 128).T)
    com["qg_col"] = np.ascontiguousarray(g("q_norm_g")[0].reshape(2, 128).T)
    com["w_uq"] = np.ascontiguousarray(g("w_uq")[0])
    com["kvg_col"] = np.ascontiguousarray(g("kv_norm_g")[0].reshape(128, 1))
    com["w_ukv"] = np.ascontiguousarray(g("w_ukv")[0])
    com["w_bs"] = np.ascontiguousarray(g("w_branch_ssm")[0])
    com["w_ba"] = np.ascontiguousarray(g("w_branch_attn")[0])
    com["w_o"] = np.ascontiguousarray(g("w_o")[0])
    com["lno_g"] = g("ln_g")[0].reshape(1, -1)
    com["lno_b"] = g("ln_b")[0].reshape(1, -1)
    xp, xs = g("x_prompt"), g("x_sample")
    maps = []
    for core in range(8):
        p, jb = core // 4, core % 4
        seqs = [xp[p], xs[2 * p], xs[2 * p + 1]]
        m = dict(com)
        for s in range(3):
            L = cfg.L[s]
            nC, nCo = cfg.nC[s], cfg.nCo[s]
            pr = _perm(L, jb)
            m[f"xr{s}"] = np.ascontiguousarray(seqs[s][pr])
            m[f"pos{s}"] = pr.astype(f).reshape(1, -1)
            ctrue_first = (np.arange(nC) + jb * nCo) % nC
            ctrue_last = (np.arange(nC) + nCo + jb * nCo) % nC
            m[f"mf{s}"] = (ctrue_last != 0).astype(f).reshape(1, -1)
            m[f"mb{s}"] = (ctrue_first != nC - 1).astype(f).reshape(1, -1)
        maps.append(m)
    return maps


def assemble(results, cfg):
    LA, LB = cfg.L[0], cfg.L[1]
    yp = np.zeros((2, LA, D_MODEL), np.float32)
    ys = np.zeros((4, LB, D_MODEL), np.float32)
    for core in range(8):
        p, jb = core // 4, core % 4
        r = results[core]
        yp[p][_own_perm(LA, jb)] = r["out0"]
        ys[2 * p][_own_perm(LB, jb)] = r["out1"]
        ys[2 * p + 1][_own_perm(LB, jb)] = r["out2"]
    return yp, ys


_NC_CACHE = {}


def kernel(**inputs):
    LA = inputs["x_prompt"].shape[1]
    LB = inputs["x_sample"].shape[1]
    cfg = Cfg(LA, LB)
    key = (LA, LB)
    if key not in _NC_CACHE:
        _NC_CACHE[key] = build(cfg)
    nc = _NC_CACHE[key]
    maps = make_in_maps(inputs, cfg)
    res = run_bass_kernel_spmd(nc, maps, core_ids=list(range(8)))
    return assemble(res.results, cfg)
```
